# Optimizing a Trainium2 kernel written in Bass

```python
import math
import jax, jax.numpy as jnp
from jax import lax
import numpy as np

D_MODEL = 2048
BATCH = 4
SEQ = 8192
DEPTH = 2
DEC_BATCH = 2
DEC_SEQ = 4096
PAST_LEN = 128

HEAD_DIM = 128
N_DIFF_HEADS = 8
DIFF_QK_DIM = HEAD_DIM // 2
N_WIN_HEADS = 8
N_WIN_KV = 2
WIN_GROUP = N_WIN_HEADS // N_WIN_KV
WINDOW = 128
BLOCK = 128
N_BUCKETS = 32
MAX_DISTANCE = 128
D_FF = 4 * D_MODEL
EPS = 1e-6
NEG = -1e30
DIFF_WIDTH = N_DIFF_HEADS * HEAD_DIM
WIN_WIDTH = N_WIN_HEADS * HEAD_DIM
MIX_WIDTH = DIFF_WIDTH + WIN_WIDTH
WIN_KV_WIDTH = N_WIN_KV * HEAD_DIM
SPLITS = (DIFF_WIDTH, DIFF_WIDTH, DIFF_WIDTH, WIN_WIDTH, WIN_KV_WIDTH, WIN_KV_WIDTH)
IN_WIDTH = sum(SPLITS)
N_HEADS_TOTAL = N_DIFF_HEADS + N_WIN_HEADS

kernel_name = "hymba_diffattn_swa_encoder"


def rmsnorm(x, g):
    xf = x.astype(jnp.float32)
    y = xf * lax.rsqrt(jnp.mean(xf * xf, axis=-1, keepdims=True) + EPS) * g.astype(jnp.float32)
    return y.astype(x.dtype)


def t5_bucket(rel):
    nb = N_BUCKETS // 2
    ret = jnp.where(rel > 0, nb, 0)
    n = jnp.abs(rel)
    max_exact = nb // 2
    nf = jnp.maximum(n, 1).astype(jnp.float32)
    large = max_exact + (jnp.log(nf / max_exact) / math.log(MAX_DISTANCE / max_exact)
                         * (nb - max_exact)).astype(jnp.int32)
    large = jnp.minimum(large, nb - 1)
    return ret + jnp.where(n < max_exact, n, large)


def diff_attention(q, k, v, lam, lambda_init, subln_g, table):
    B, S = q.shape[0], q.shape[1]
    nblk = S // BLOCK
    scale = DIFF_QK_DIM ** -0.5
    k_pos = jnp.arange(S)
    qb = q.reshape(B, nblk, BLOCK, N_DIFF_HEADS, 2, DIFF_QK_DIM).transpose(1, 0, 2, 3, 4, 5)

    def one_block(args):
        qblk, n = args
        q_pos = n * BLOCK + jnp.arange(BLOCK)
        bias = table[t5_bucket(k_pos[None, :] - q_pos[:, None])]
        bias = bias.astype(jnp.float32).transpose(2, 0, 1)[:, None]
        s = jnp.einsum('bqhcd,bkhcd->bhcqk', qblk, k).astype(jnp.float32) * scale + bias
        p = jax.nn.softmax(s, axis=-1)
        w = p[:, :, 0] - lam * p[:, :, 1]
        return jnp.einsum('bhqk,bkhe->bqhe', w.astype(v.dtype), v)

    o = lax.map(one_block, (qb, jnp.arange(nblk)))
    o = o.transpose(1, 0, 2, 3, 4).reshape(B, S, N_DIFF_HEADS, HEAD_DIM)
    o = rmsnorm(o, subln_g) * (1.0 - lambda_init)
    return o.reshape(B, S, DIFF_WIDTH)


def window_attention(q, k, v, sink, table):
    B, S = q.shape[0], q.shape[1]
    nblk = S // BLOCK
    scale = HEAD_DIM ** -0.5
    qb = q.reshape(B, nblk, BLOCK, N_WIN_KV, WIN_GROUP, HEAD_DIM)

    def band(t):
        tp = jnp.pad(t, ((0, 0), (WINDOW, WINDOW), (0, 0), (0, 0)))
        tp = tp.reshape(B, nblk + 2, BLOCK, N_WIN_KV, HEAD_DIM)
        return jnp.concatenate([tp[:, :-2], tp[:, 1:-1], tp[:, 2:]], axis=2)

    kb, vb = band(k), band(v)
    rel = jnp.arange(3 * BLOCK)[None, :] - WINDOW - jnp.arange(BLOCK)[:, None]
    key_pos = jnp.arange(nblk)[:, None] * BLOCK + jnp.arange(3 * BLOCK)[None, :] - WINDOW
    mask = (jnp.abs(rel) <= WINDOW)[None] & ((key_pos >= 0) & (key_pos < S))[:, None, :]
    bias = table[t5_bucket(rel)].astype(jnp.float32)
    bias = bias.transpose(2, 0, 1).reshape(N_WIN_KV, WIN_GROUP, BLOCK, 3 * BLOCK)
    s = jnp.einsum('bnqkgd,bnjkd->bnkgqj', qb, kb).astype(jnp.float32) * scale + bias
    s = jnp.where(mask[None, :, None, None], s, NEG)
    sink_b = sink.astype(jnp.float32).reshape(N_WIN_KV, WIN_GROUP)[None, None, :, :, None, None]
    m = jnp.maximum(jnp.max(s, axis=-1, keepdims=True), sink_b)
    e = jnp.exp(s - m)
    p = e / (jnp.sum(e, axis=-1, keepdims=True) + jnp.exp(sink_b - m))
    o = jnp.einsum('bnkgqj,bnjkd->bnqkgd', p.astype(v.dtype), vb)
    return o.reshape(B, S, WIN_WIDTH)


def trunk(x, rel_bias, norm1_g, w_in, lambda_q1, lambda_k1, lambda_q2, lambda_k2,
          diff_subln_g, sink_logit, w_out, norm2_g, w_ff_in, w_ff_out, final_norm_g):
    B, S = x.shape[0], x.shape[1]
    table_a = rel_bias[:, :N_DIFF_HEADS]
    table_b = rel_bias[:, N_DIFF_HEADS:]
    cuts = [int(c) for c in np.cumsum(SPLITS)[:-1]]
    for l in range(DEPTH):
        lambda_init = 0.8 - 0.6 * math.exp(-0.3 * l)
        lam = (jnp.exp(jnp.sum(lambda_q1[l].astype(jnp.float32) * lambda_k1[l].astype(jnp.float32)))
               - jnp.exp(jnp.sum(lambda_q2[l].astype(jnp.float32) * lambda_k2[l].astype(jnp.float32)))
               + lambda_init)
        h = rmsnorm(x, norm1_g[l])
        proj = h @ w_in[l]
        qa, ka, va, qw, kw, vw = jnp.split(proj, cuts, axis=-1)
        qa = qa.reshape(B, S, N_DIFF_HEADS, 2, DIFF_QK_DIM)
        ka = ka.reshape(B, S, N_DIFF_HEADS, 2, DIFF_QK_DIM)
        va = va.reshape(B, S, N_DIFF_HEADS, HEAD_DIM)
        qw = qw.reshape(B, S, N_WIN_HEADS, HEAD_DIM)
        kw = kw.reshape(B, S, N_WIN_KV, HEAD_DIM)
        vw = vw.reshape(B, S, N_WIN_KV, HEAD_DIM)
        oa = diff_attention(qa, ka, va, lam, lambda_init, diff_subln_g[l], table_a)
        ow = window_attention(qw, kw, vw, sink_logit[l], table_b)
        x = x + jnp.concatenate([oa, ow], axis=-1) @ w_out[l]
        h2 = rmsnorm(x, norm2_g[l])
        x = x + jnp.square(jax.nn.relu(h2 @ w_ff_in[l])) @ w_ff_out[l]
    return rmsnorm(x, final_norm_g)


def setup_inputs(seed: int = 0) -> dict:
    key = jax.random.key(seed)
    ks = jax.random.split(key, 18)
    f32 = jnp.float32
    nrm = lambda k, shape, s: jax.random.normal(k, shape, f32) * s
    return {
        "x_prompt": nrm(ks[0], (BATCH, SEQ, D_MODEL), 1.0),
        "x_sample": nrm(ks[1], (DEC_BATCH, DEC_SEQ, D_MODEL), 1.0),
        "rel_bias": nrm(ks[2], (N_BUCKETS, N_HEADS_TOTAL), 0.5),
        "norm1_g": 1.0 + nrm(ks[3], (DEPTH, D_MODEL), 0.05),
        "w_in": nrm(ks[4], (DEPTH, D_MODEL, IN_WIDTH), D_MODEL ** -0.5),
        "lambda_q1": nrm(ks[5], (DEPTH, DIFF_QK_DIM), 0.1),
        "lambda_k1": nrm(ks[6], (DEPTH, DIFF_QK_DIM), 0.1),
        "lambda_q2": nrm(ks[7], (DEPTH, DIFF_QK_DIM), 0.1),
        "lambda_k2": nrm(ks[8], (DEPTH, DIFF_QK_DIM), 0.1),
        "diff_subln_g": 1.0 + nrm(ks[9], (DEPTH, HEAD_DIM), 0.05),
        "sink_logit": nrm(ks[10], (DEPTH, N_WIN_HEADS), 0.5),
        "w_out": nrm(ks[11], (DEPTH, MIX_WIDTH, D_MODEL), MIX_WIDTH ** -0.5),
        "norm2_g": 1.0 + nrm(ks[12], (DEPTH, D_MODEL), 0.05),
        "w_ff_in": nrm(ks[13], (DEPTH, D_MODEL, D_FF), D_MODEL ** -0.5),
        "w_ff_out": nrm(ks[14], (DEPTH, D_FF, D_MODEL), D_FF ** -0.5),
        "final_norm_g": 1.0 + nrm(ks[15], (D_MODEL,), 0.05),
    }


def reference(x_prompt, x_sample, rel_bias, norm1_g, w_in, lambda_q1, lambda_k1, lambda_q2,
              lambda_k2, diff_subln_g, sink_logit, w_out, norm2_g, w_ff_in, w_ff_out, final_norm_g):
    y_prompt = trunk(x_prompt, rel_bias, norm1_g, w_in, lambda_q1, lambda_k1, lambda_q2, lambda_k2,
                     diff_subln_g, sink_logit, w_out, norm2_g, w_ff_in, w_ff_out, final_norm_g)
    y_sample = trunk(x_sample, rel_bias, norm1_g, w_in, lambda_q1, lambda_k1, lambda_q2, lambda_k2,
                     diff_subln_g, sink_logit, w_out, norm2_g, w_ff_in, w_ff_out, final_norm_g)
    return (y_prompt, y_sample)
```

```python
import math
import numpy as np
import ml_dtypes
import concourse.bass as bass
import concourse.mybir as mybir
from concourse.bass_utils import run_bass_kernel_spmd

F32 = mybir.dt.float32
BF16 = mybir.dt.bfloat16
AF = mybir.ActivationFunctionType
ALU = mybir.AluOpType

S = 8192
D = 2048
NT = S // 512
NKB = S // 128
DEBUG_SCRATCH = False
PHASES = None
BSTOP = 9
DEPTH = 2
EPS = 1e-6
NEGM = -30000.0
VECW = 1408
COMPUTE = ("pe", "act", "dve", "pool")


class Buf:
    __slots__ = ("name", "w", "r", "rd", "sem", "cnt")

    def __init__(self, name):
        self.name = name
        self.w = None
        self.r = {}
        self.rd = []
        self.sem = None
        self.cnt = 0


_BUFS = {}


def mkbuf(name):
    b = _BUFS.get(name)
    if b is None:
        b = _BUFS[name] = Buf(name)
    return b


class Op:
    __slots__ = ("eng", "fn", "deps", "marked", "cnt", "chan", "is_dma")

    def __init__(self, eng, fn, is_dma=False, chan=None):
        self.eng = eng
        self.fn = fn
        self.deps = ()
        self.marked = False
        self.cnt = 0
        self.chan = chan
        self.is_dma = is_dma


class Prog:
    def __init__(self, nc):
        self.nc = nc
        self.streams = {e: [] for e in ("pe", "act", "dve", "pool", "sp")}
        self.sems = {}
        self.chans = []
        self.pending = {e: [] for e in self.streams}

    def _add(self, op, reads, writes):
        deps = set()
        for b in reads:
            if b.w is not None:
                deps.add(b.w)
        for b in writes:
            if b.w is not None:
                deps.add(b.w)
            deps.update(b.r.values())
            deps.update(b.rd)
        pend = self.pending[op.eng]
        if pend:
            deps.update(pend)
            self.pending[op.eng] = []
        deps.discard(op)
        op.deps = tuple(deps)
        for b in writes:
            b.w = op
            b.r = {}
            b.rd = []
        for b in reads:
            if b.w is not op:
                if op.is_dma:
                    b.rd.append(op)
                else:
                    b.r[op.eng] = op
        self.streams[op.eng].append(op)
        return op

    def op(self, eng, fn, reads=(), writes=()):
        return self._add(Op(eng, fn), reads, writes)

    def dma(self, queue, out, in_, chan, reads=(), writes=()):
        if chan.sem is None:
            chan.sem = {}
        if queue not in chan.sem:
            chan.sem[queue] = [self.nc.alloc_semaphore("c_" + queue + "_" + chan.name), 0]
            self.chans.append(chan.sem[queue])
        ent = chan.sem[queue]
        o = Op(queue, lambda e: e.dma_start(out=out, in_=in_), is_dma=True, chan=ent)
        self._add(o, reads, writes)
        ent[1] += 1
        o.cnt = ent[1]
        return o

    def barrier(self):
        last = []
        seen = set()
        for e, s in self.streams.items():
            got = False
            for o in reversed(s):
                if o.is_dma:
                    if id(o.chan) not in seen:
                        seen.add(id(o.chan))
                        last.append(o)
                elif not got:
                    got = True
                    last.append(o)
        for e in self.pending:
            self.pending[e] = list(last)

    def finalize(self):
        for e, s in self.streams.items():
            for o in s:
                for d in o.deps:
                    if d.is_dma:
                        continue
                    if d.eng == "pe" and o.eng == "pe" and not o.is_dma:
                        continue
                    d.marked = True
        for e in COMPUTE:
            c = 0
            for o in self.streams[e]:
                if (not o.is_dma) and o.marked:
                    c += 1
                    o.cnt = c
            self.sems[e] = self.nc.alloc_semaphore("e_" + e)

    def replay(self, ename, eng):
        seen = {}
        for o in self.streams[ename]:
            need = {}
            for d in o.deps:
                if d.is_dma:
                    key = d.chan[0]
                    val = 16 * d.cnt
                else:
                    if d.eng == "pe" and ename == "pe" and not o.is_dma:
                        continue
                    key = self.sems[d.eng]
                    val = d.cnt
                k = id(key)
                if k not in need or need[k][1] < val:
                    need[k] = (key, val)
            for k, (sem, val) in need.items():
                if seen.get(k, 0) >= val:
                    continue
                seen[k] = val
                eng.wait_ge(sem, val)
            ins = o.fn(eng)
            if o.is_dma:
                ins.then_inc(o.chan[0], 16)
            elif o.marked:
                ins.then_inc(self.sems[ename], 1)

    def run(self):
        self.finalize()
        nc = self.nc
        with nc.Block() as block:
            @block.tensor
            def _(e):
                self.replay("pe", e)

            @block.scalar
            def _(e):
                self.replay("act", e)

            @block.vector
            def _(e):
                self.replay("dve", e)

            @block.gpsimd
            def _(e):
                self.replay("pool", e)

            @block.sync
            def _(e):
                self.replay("sp", e)
                for ch in self.chans:
                    if ch[1]:
                        e.wait_ge(ch[0], 16 * ch[1])


class Carver:
    def __init__(self, ap):
        self.ap = ap
        self.off = 0

    def take(self, n):
        a = self.ap[:, self.off:self.off + n]
        self.off += n
        assert self.off <= self.ap.shape[1], (self.off, self.ap.shape)
        return a


class Ring:
    def __init__(self, P, slots, name, depth=6):
        self.P = P
        self.slots = slots
        self.bufs = [mkbuf(f"{name}{i}") for i in range(len(slots))]
        self.uses = []
        self.loaded = 0
        self.depth = depth

    def add(self, src, view):
        self.uses.append((src, view))

    def get(self, k):
        R = len(self.slots)
        while self.loaded < len(self.uses) and self.loaded <= k + self.depth:
            u = self.loaded
            src, view = self.uses[u]
            self.P.dma("sp", view(self.slots[u % R]), src, self.bufs[u % R], writes=[self.bufs[u % R]])
            self.loaded += 1
        src, view = self.uses[k]
        return view(self.slots[k % R]), self.bufs[k % R]


def v16(ap):
    return ap.rearrange("p (k n) -> p k n", k=16)


def v4(ap):
    return ap.rearrange("p (k n) -> p k n", k=4)


def v4h(ap):
    return ap[:, 0:1024].rearrange("p (k n) -> p k n", k=4)


def build_program():
    _BUFS.clear()
    nc = bass.Bass("TRN2", target_bir_lowering=False)
    P = Prog(nc)

    def din(name, shape, dt=F32):
        return nc.dram_tensor(name, list(shape), dt, kind="ExternalInput").ap()

    def dscr(name, shape, dt):
        kind = "ExternalOutput" if (DEBUG_SCRATCH and name in ("QT", "KT", "VA", "VW", "AOT", "XR", "VEC")) else "Internal"
        return nc.dram_tensor(name, list(shape), dt, kind=kind).ap()

    x_d = din("x", [S, D])
    mv_d = din("mv", [128, 1])
    relb_d = din("rel_bias", [32, 16])
    n1g_d = din("norm1_g", [DEPTH, D])
    win_d = din("w_in", [DEPTH, D, 4608])
    lq1_d = din("lambda_q1", [DEPTH, 64])
    lk1_d = din("lambda_k1", [DEPTH, 64])
    lq2_d = din("lambda_q2", [DEPTH, 64])
    lk2_d = din("lambda_k2", [DEPTH, 64])
    sub_d = din("diff_subln_g", [DEPTH, 128])
    sink_d = din("sink_logit", [DEPTH, 8])
    wout_d = din("w_out", [DEPTH, D, D])
    n2g_d = din("norm2_g", [DEPTH, D])
    wf1_d = din("w_ff_in", [DEPTH, D, 8192])
    wf2_d = din("w_ff_out", [DEPTH, 8192, D])
    fng_d = din("final_norm_g", [1, D])
    idb_d = din("idb", [128, 128], BF16)
    J_d = din("jmat", [128, 128])
    ohd_d = din("ohd", [32, 1280])
    ohw_d = din("ohw", [64, 1280])
    y_d = nc.dram_tensor("y", [S, D], F32, kind="ExternalOutput").ap()

    WA = dscr("WA", [DEPTH, 26, 128, 16, 128], BF16)
    WV = dscr("WV", [DEPTH, 3, 4, 128, 4, 512], BF16)
    WO = dscr("WO", [DEPTH, 4, 4, 128, 4, 512], BF16)
    W1 = dscr("W1", [DEPTH, 64, 128, 16, 128], BF16)
    W2 = dscr("W2", [DEPTH, 4, 16, 128, 4, 512], BF16)
    QT = dscr("QT", [16, 128, S], BF16)
    KT = dscr("KT", [10, 128, S], BF16)
    VA = dscr("VA", [S, 1024], BF16)
    VW = dscr("VW", [S, 256], BF16)
    AOT = dscr("AOT", [16, 128, S], BF16)
    XR = dscr("XR", [S, D], F32)
    VEC = dscr("VEC", [16, VECW], F32)

    NF = 20480
    NB = 57600
    arF = nc.alloc_sbuf_tensor("arF", [128, NF], F32).ap()
    arB = nc.alloc_sbuf_tensor("arB", [128, NB], BF16).ap()
    idb = nc.alloc_sbuf_tensor("idb_s", [128, 128], BF16).ap()
    Jm = nc.alloc_sbuf_tensor("J_s", [128, 128], F32).ap()
    mvc = nc.alloc_sbuf_tensor("mv_s", [128, 1], F32).ap()
    zc = nc.alloc_sbuf_tensor("zc_s", [128, 1], F32).ap()
    clo = nc.alloc_sbuf_tensor("clo_s", [128, 16], F32).ap()
    chi = nc.alloc_sbuf_tensor("chi_s", [128, 16], F32).ap()
    clom = nc.alloc_sbuf_tensor("clom_s", [128, 16], F32).ap()
    chim = nc.alloc_sbuf_tensor("chim_s", [128, 16], F32).ap()
    banks = [nc.alloc_psum_tensor(f"bk{i}", [128, 512], F32).ap() for i in range(8)]
    bankB = [mkbuf(f"bank{i}") for i in range(8)]
    Bidb, BJ, Bmv, Bzc, Bc = mkbuf("idb"), mkbuf("J"), mkbuf("mv"), mkbuf("zc"), mkbuf("cc")

    def setup():
        Bw = mkbuf("wprep")
        qcols = [u * 128 for u in range(8)] + [1024 + u * 128 for u in range(8)] + \
                [3072 + u * 128 for u in range(8)] + [4096, 4224]
        for l in range(DEPTH):
            wi = win_d[l].rearrange("(kc p) n -> p kc n", p=128)
            for u, c0 in enumerate(qcols):
                P.dma("pool", WA[l, u], wi[:, :, c0:c0 + 128], Bw)
            for b in range(3):
                c0 = 2048 + b * 512 if b < 2 else 4352
                w = 512 if b < 2 else 256
                for kg in range(4):
                    P.dma("pool", WV[l, b, kg][:, :, 0:w], wi[:, kg * 4:(kg + 1) * 4, c0:c0 + w], Bw)
            wo = wout_d[l].rearrange("(kc p) n -> p kc n", p=128)
            for b in range(4):
                for kg in range(4):
                    P.dma("pool", WO[l, b, kg], wo[:, kg * 4:(kg + 1) * 4, b * 512:(b + 1) * 512], Bw)
            w1 = wf1_d[l].rearrange("(kc p) n -> p kc n", p=128)
            for u in range(64):
                P.dma("pool", W1[l, u], w1[:, :, u * 128:(u + 1) * 128], Bw)
            w2 = wf2_d[l].rearrange("(kc p) n -> p kc n", p=128)
            for b in range(4):
                for kg in range(16):
                    P.dma("pool", W2[l, b, kg], w2[:, kg * 4:(kg + 1) * 4, b * 512:(b + 1) * 512], Bw)
        P.dma("sp", idb, idb_d, Bidb, writes=[Bidb])
        P.dma("sp", Jm, J_d, BJ, writes=[BJ])
        P.dma("sp", mvc, mv_d, Bmv, writes=[Bmv])
        P.op("dve", lambda e: e.memset(zc, 0.0), writes=[Bzc])
        P.dma("sp", clo, relb_d[15:16, :].partition_broadcast(128), Bc, writes=[Bc])
        Bc2 = mkbuf("cc2")
        P.dma("sp", chi, relb_d[31:32, :].partition_broadcast(128), Bc2, writes=[Bc2])
        P.op("dve", lambda e: e.tensor_scalar(out=clom, in0=clo, scalar1=mvc, scalar2=None, op0=ALU.add),
             reads=[Bc, Bmv], writes=[mkbuf("clom")])
        P.op("dve", lambda e: e.tensor_scalar(out=chim, in0=chi, scalar1=mvc, scalar2=None, op0=ALU.add),
             reads=[Bc2, Bmv], writes=[mkbuf("chim")])
        cf = Carver(arF)
        tab = cf.take(16)[0:64, :]
        ohd = cf.take(1280)[0:32, :]
        ohw = cf.take(1280)[0:64, :]
        vd = cf.take(1280)[0:8, :]
        vw = cf.take(1280)[0:8, :]
        Bt, Bohd, Bohw, Bvd, Bvw = mkbuf("tab"), mkbuf("ohd"), mkbuf("ohw"), mkbuf("vd"), mkbuf("vw")
        P.op("dve", lambda e: e.memset(tab, 0.0), writes=[Bt])
        P.op("dve", lambda e: e.memset(tab[32:33, :], NEGM), writes=[Bt])
        P.dma("sp", tab[0:32, :], relb_d, Bt, writes=[Bt])
        P.dma("sp", ohd, ohd_d, Bohd, writes=[Bohd])
        P.dma("sp", ohw, ohw_d, Bohw, writes=[Bohw])
        for pc in range(3):
            cs = slice(pc * 512, min(1280, (pc + 1) * 512))
            n = cs.stop - cs.start
            P.op("pe", lambda e, cs=cs, n=n: e.matmul(banks[0][0:8, 0:n], lhsT=tab[0:32, 0:8], rhs=ohd[:, cs],
                                                     start=True, stop=True), reads=[Bt, Bohd], writes=[bankB[0]])
            P.op("dve", lambda e, cs=cs, n=n: e.tensor_copy(out=vd[:, cs], in_=banks[0][0:8, 0:n]),
                 reads=[bankB[0]], writes=[Bvd])
            P.op("pe", lambda e, cs=cs, n=n: e.matmul(banks[1][0:8, 0:n], lhsT=tab[0:64, 8:16], rhs=ohw[:, cs],
                                                     start=True, stop=True), reads=[Bt, Bohw], writes=[bankB[1]])
            P.op("dve", lambda e, cs=cs, n=n: e.tensor_copy(out=vw[:, cs], in_=banks[1][0:8, 0:n]),
                 reads=[bankB[1]], writes=[Bvw])
        P.dma("sp", VEC[0:8, 0:1280], vd, Bvd, reads=[Bvd])
        P.dma("sp", VEC[8:16, 0:1280], vw, Bvw, reads=[Bvw])
        P.barrier()

    def rms_rstd(xin, n, scr_sq, Bscr, st, Bst, reads):
        P.op("act", lambda e: e.activation(out=scr_sq, in_=xin, func=AF.Square, accum_out=st[:, 0:1]),
             reads=reads, writes=[Bscr, Bst])
        P.op("dve", lambda e: e.tensor_scalar(out=st[:, 1:2], in0=st[:, 0:1], scalar1=1.0 / n, scalar2=EPS,
                                              op0=ALU.mult, op1=ALU.add), reads=[Bst], writes=[Bst])
        P.op("act", lambda e: e.activation(out=st[:, 2:3], in_=st[:, 1:2], func=AF.Sqrt), reads=[Bst], writes=[Bst])
        P.op("dve", lambda e: e.reciprocal(out=st[:, 3:4], in_=st[:, 2:3]), reads=[Bst], writes=[Bst])
        return st[:, 3:4]

    def transpose_block(src, Bsrc, dstT, BdstT, ts, tbank_idx):
        for g8 in range(2):
            bk = tbank_idx[g8]
            tb = banks[bk].bitcast(BF16)
            for k in range(8):
                kc = g8 * 8 + k
                P.op("pe", lambda e, tb=tb, k=k, kc=kc: e.transpose(out=tb[:, k * 128:(k + 1) * 128],
                                                                   in_=src[:, kc * 128:(kc + 1) * 128], identity=idb),
                     reads=[Bsrc, Bidb], writes=[bankB[bk]])
            eng = "dve" if g8 == 0 else "act"
            dst = dstT[:, g8 * 8:(g8 + 1) * 8, ts * 128:(ts + 1) * 128]
            srcv = tb.rearrange("p (k n) -> p k n", k=8)
            if eng == "dve":
                P.op("dve", lambda e, dst=dst, srcv=srcv: e.tensor_copy(out=dst, in_=srcv), reads=[bankB[bk]], writes=[BdstT])
            else:
                P.op("act", lambda e, dst=dst, srcv=srcv: e.copy(out=dst, in_=srcv), reads=[bankB[bk]], writes=[BdstT])

    def phase_A(l):
        x_src = x_d if l == 0 else XR
        cf = Carver(arF)
        cb = Carver(arB)
        xA = [cf.take(4 * D).rearrange("p (t d) -> p t d", t=4) for _ in range(2)]
        xB = [[mkbuf(f"xA{s}{t}") for t in range(4)] for s in range(2)]
        gbc = cf.take(D)
        Bg = mkbuf("gbcA")
        stt = [[cf.take(4) for t in range(4)] for s in range(2)]
        Bst = [[mkbuf(f"stA{s}{t}") for t in range(4)] for s in range(2)]
        hb = [cb.take(D) for _ in range(2)]
        Bh = [mkbuf(f"hA{s}") for s in range(2)]
        hT = [cb.take(16 * 512).rearrange("p (k n) -> p k n", k=16) for _ in range(2)]
        BhT = [mkbuf(f"hTA{s}") for s in range(2)]
        ring = Ring(P, [cb.take(2048) for _ in range(8)], "rgA")
        stage = [cb.take(512) for _ in range(4)]
        Bstage = [mkbuf(f"stgA{s}") for s in range(4)]
        stagev = [cb.take(512) for _ in range(4)]
        Bstagev = [mkbuf(f"stgvA{s}") for s in range(4)]
        junk = cb.take(D)
        Bjunk = mkbuf("junkA")

        P.dma("sp", gbc, n1g_d[l:l + 1, :].partition_broadcast(128), Bg, writes=[Bg])
        for i in range(NT):
            for u in range(26):
                ring.add(WA[l, u], v16)
            for b in range(3):
                for kg in range(4):
                    if b < 2:
                        ring.add(WV[l, b, kg], v4)
                    else:
                        ring.add(WV[l, b, kg][:, :, 0:256], lambda ap: ap[:, 0:1024].rearrange("p (k n) -> p k n", k=4))

        def load_x(i):
            s = i % 2
            for ts in range(4):
                r0 = i * 512 + ts * 128
                P.dma("sp", xA[s][:, ts, :], x_src[r0:r0 + 128, :], xB[s][ts], writes=[xB[s][ts]])

        load_x(0)
        ucount = 0
        for i in range(NT):
            s = i % 2
            if i + 1 < NT:
                load_x(i + 1)
            for ts in range(4):
                xin = xA[s][:, ts, :]
                rstd = rms_rstd(xin, D, junk, Bjunk, stt[s][ts], Bst[s][ts], [xB[s][ts]])
                hs = ts % 2
                P.op("dve", lambda e, xin=xin, rstd=rstd, hs=hs: e.scalar_tensor_tensor(
                    out=hb[hs], in0=xin, scalar=rstd, in1=gbc, op0=ALU.mult, op1=ALU.mult),
                    reads=[xB[s][ts], Bst[s][ts], Bg], writes=[Bh[hs]])
                transpose_block(hb[hs], Bh[hs], hT[s], BhT[s], ts, (0, 1))
            for u in range(26):
                wu, Bwu = ring.get(ucount)
                ucount += 1
                fb = 2 + (u % 2)
                for kc in range(16):
                    P.op("pe", lambda e, wu=wu, kc=kc, fb=fb, s=s: e.matmul(
                        banks[fb], lhsT=wu[:, kc, :], rhs=hT[s][:, kc, :], start=(kc == 0), stop=(kc == 15)),
                        reads=[Bwu, BhT[s]], writes=[bankB[fb]])
                sg = u % 4
                P.op("act", lambda e, sg=sg, fb=fb: e.copy(out=stage[sg], in_=banks[fb]),
                     reads=[bankB[fb]], writes=[Bstage[sg]])
                if u < 8:
                    dst = QT[u]
                elif u < 16:
                    dst = KT[u - 8]
                elif u < 24:
                    dst = QT[8 + u - 16]
                else:
                    dst = KT[8 + u - 24]
                P.dma("pool", dst[:, i * 512:(i + 1) * 512], stage[sg], Bstage[sg], reads=[Bstage[sg]])
            for b in range(3):
                w = 512 if b < 2 else 256
                for kg in range(4):
                    wu, Bwu = ring.get(ucount)
                    ucount += 1
                    for kci in range(4):
                        kc = kg * 4 + kci
                        for ts in range(4):
                            P.op("pe", lambda e, wu=wu, kci=kci, kc=kc, ts=ts, w=w, s=s: e.matmul(
                                banks[4 + ts][:, 0:w], lhsT=hT[s][:, kc, ts * 128:(ts + 1) * 128], rhs=wu[:, kci, :],
                                start=(kc == 0), stop=(kc == 15)),
                                reads=[Bwu, BhT[s]], writes=[bankB[4 + ts]])
                for ts in range(4):
                    eng = "dve" if ts % 2 == 0 else "act"
                    if eng == "dve":
                        P.op("dve", lambda e, ts=ts, w=w: e.tensor_copy(out=stagev[ts][:, 0:w], in_=banks[4 + ts][:, 0:w]),
                             reads=[bankB[4 + ts]], writes=[Bstagev[ts]])
                    else:
                        P.op("act", lambda e, ts=ts, w=w: e.copy(out=stagev[ts][:, 0:w], in_=banks[4 + ts][:, 0:w]),
                             reads=[bankB[4 + ts]], writes=[Bstagev[ts]])
                    r0 = i * 512 + ts * 128
                    dst = VA[r0:r0 + 128, b * 512:(b + 1) * 512] if b < 2 else VW[r0:r0 + 128, :]
                    P.dma("pool", dst, stagev[ts][:, 0:w], Bstagev[ts], reads=[Bstagev[ts]])
        P.barrier()

    def phase_B(l):
        lam_init = 0.8 - 0.6 * math.exp(-0.3 * l)
        cf = Carver(arF)
        cb = Carver(arB)
        strips = [cf.take(1152) for _ in range(8)]
        Bstrip = [mkbuf(f"strip{h}") for h in range(8)]
        wst = [[cf.take(512).rearrange("p (h q) -> p h q", h=4) for j in range(3)] for g in range(2)]
        Bwst = [[mkbuf(f"wst{g}{j}") for j in range(3)] for g in range(2)]
        hk = cf.take(1152)
        Bhk = mkbuf("hk")
        Sn = [cf.take(512) for _ in range(2)]
        BSn = [mkbuf(f"Sn{i}") for i in range(2)]
        Rc = [cf.take(512) for _ in range(2)]
        BRc = [mkbuf(f"Rc{i}") for i in range(2)]
        O0 = cf.take(512)
        BO0 = mkbuf("O0")
        T1 = cf.take(512)
        BT1 = mkbuf("T1")
        Oc = [cf.take(512) for _ in range(2)]
        BOc = [mkbuf(f"Oc{i}") for i in range(2)]
        Osq = cf.take(512)
        BOsq = mkbuf("Osq")
        Var = cf.take(512)
        BVar = mkbuf("Var")
        onesf = cf.take(128)
        Bonesf = mkbuf("onesf")
        sinkbc = [cf.take(512).rearrange("p (h q) -> p h q", h=4) for g in range(2)]
        Bsinkbc = [mkbuf(f"sinkbc{g}") for g in range(2)]
        junkf = cf.take(128)
        Bjunkf = mkbuf("junkfB")
        gsubc = cf.take(1)
        Bgsub = mkbuf("gsub")
        sinke = cf.take(8)
        Bsink = mkbuf("sinke")
        lt = cf.take(64 * 4).rearrange("p (a d) -> p a d", a=4)
        Blt = mkbuf("lt")
        lsm = cf.take(8)
        Blsm = mkbuf("lsm")
        KTp = [[cb.take(S) for _ in range(2)] for c in range(2)]
        BKT = [mkbuf(f"KTb{i}") for i in range(2)]
        Vb = [cb.take(NKB * 128).rearrange("p (k e) -> p k e", k=NKB) for _ in range(2)]
        BV = [mkbuf(f"Vb{i}") for i in range(2)]
        QTt = [cb.take(512) for _ in range(3)]
        BQT = [mkbuf(f"QTt{i}") for i in range(3)]
        QW = [cb.take(4 * 512).rearrange("p (h q) -> p h q", h=4) for _ in range(2)]
        BQW = [mkbuf(f"QW{i}") for i in range(2)]
        E = [cb.take(512) for _ in range(3)]
        BE = [mkbuf(f"E{i}") for i in range(3)]
        OBT = [cb.take(512) for _ in range(2)]
        BOB = [mkbuf(f"OB{i}") for i in range(2)]
        onesb = cb.take(128)
        Bonesb = mkbuf("onesb")

        for a, src in enumerate((lq1_d, lk1_d, lq2_d, lk2_d)):
            P.dma("sp", lt[:, a, :], src[l:l + 1, :].partition_broadcast(128), Blt, writes=[Blt])
        P.op("dve", lambda e: e.scalar_tensor_tensor(out=junkf[:, 0:64], in0=lt[:, 0, :], scalar=1.0, in1=lt[:, 1, :],
                                                     op0=ALU.mult, op1=ALU.mult, accum_out=lsm[:, 0:1]),
             reads=[Blt], writes=[Bjunkf, Blsm])
        P.op("dve", lambda e: e.scalar_tensor_tensor(out=junkf[:, 0:64], in0=lt[:, 2, :], scalar=1.0, in1=lt[:, 3, :],
                                                     op0=ALU.mult, op1=ALU.mult, accum_out=lsm[:, 1:2]),
             reads=[Blt], writes=[Bjunkf, Blsm])
        P.op("act", lambda e: e.activation(out=lsm[:, 2:4], in_=lsm[:, 0:2], func=AF.Exp), reads=[Blsm], writes=[Blsm])
        P.op("dve", lambda e: e.tensor_tensor(out=lsm[:, 4:5], in0=lsm[:, 2:3], in1=lsm[:, 3:4], op=ALU.subtract),
             reads=[Blsm], writes=[Blsm])
        P.op("dve", lambda e: e.tensor_scalar(out=lsm[:, 5:6], in0=lsm[:, 4:5], scalar1=-1.0, scalar2=-lam_init,
                                              op0=ALU.mult, op1=ALU.add), reads=[Blsm], writes=[Blsm])
        neglam = lsm[:, 5:6]
        P.dma("sp", gsubc, sub_d[l:l + 1, :].rearrange("o e -> e o"), Bgsub, writes=[Bgsub])
        P.op("dve", lambda e: e.tensor_scalar(out=gsubc, in0=gsubc, scalar1=1.0 - lam_init, scalar2=None, op0=ALU.mult),
             reads=[Bgsub], writes=[Bgsub])
        P.dma("sp", sinke, sink_d[l:l + 1, :].partition_broadcast(128), Bsink, writes=[Bsink])
        P.op("act", lambda e: e.activation(out=sinke, in_=sinke, func=AF.Exp), reads=[Bsink], writes=[Bsink])
        P.op("dve", lambda e: e.memset(onesf, 1.0), writes=[Bonesf])
        P.op("dve", lambda e: e.memset(onesb, 1.0), writes=[Bonesb])
        for g in range(2):
            P.op("dve", lambda e, g=g: e.memset(sinkbc[g], 0.0), writes=[Bsinkbc[g]])
            for hh in range(4):
                hd = 4 * g + hh
                P.op("dve", lambda e, g=g, hh=hh, hd=hd: e.tensor_scalar(out=sinkbc[g][:, hh, :], in0=sinkbc[g][:, hh, :],
                                                                       scalar1=sinke[:, hd:hd + 1], scalar2=None, op0=ALU.add),
                     reads=[Bsink, Bsinkbc[g]], writes=[Bsinkbc[g]])
        if BSTOP < 1:
            P.barrier()
            return
        for hd in range(16):
            src = bass.AP(VEC.tensor, hd * VECW, [[1, 128], [1, 1152]])
            P.dma("sp", hk, src, Bhk, writes=[Bhk])
            if hd < 8:
                for pc in range(3):
                    bk = pc
                    P.op("pe", lambda e, pc=pc, bk=bk: e.matmul(banks[bk][:, 0:384], lhsT=Jm, rhs=hk[:, pc * 384:(pc + 1) * 384],
                                                              start=True, stop=True), reads=[BJ, Bhk], writes=[bankB[bk]])
                    P.op("dve", lambda e, pc=pc, bk=bk, hd=hd: e.tensor_copy(out=strips[hd][:, pc * 384:(pc + 1) * 384],
                                                                           in_=banks[bk][:, 0:384]),
                         reads=[bankB[bk]], writes=[Bstrip[hd]])
            else:
                g, hh = (hd - 8) // 4, (hd - 8) % 4
                P.op("pe", lambda e: e.matmul(banks[3][:, 0:384], lhsT=Jm, rhs=hk[:, 384:768], start=True, stop=True),
                     reads=[BJ, Bhk], writes=[bankB[3]])
                for jj, j in enumerate((-1, 0, 1)):
                    c0 = 128 - 128 * j
                    P.op("dve", lambda e, g=g, hh=hh, jj=jj, c0=c0: e.tensor_copy(out=wst[g][jj][:, hh, :],
                                                                               in_=banks[3][:, c0:c0 + 128]),
                         reads=[bankB[3]], writes=[Bwst[g][jj]])
        if BSTOP < 2:
            P.barrier()
            return

        for sl_ in range(2):
            P.op("pool", lambda e, sl_=sl_: e.memset(KTp[0][sl_][64:128, :], 0.0), writes=[BKT[sl_]])
            P.op("pool", lambda e, sl_=sl_: e.memset(KTp[1][sl_][0:64, :], 0.0), writes=[BKT[sl_]])

        def load_head(hh_):
            sl = hh_ % 2
            if hh_ < 8:
                P.dma("sp", KTp[0][sl][0:64, :], KT[hh_][0:64, :], BKT[sl], writes=[BKT[sl]])
                P.dma("sp", KTp[1][sl][64:128, :], KT[hh_][64:128, :], BKT[sl], writes=[BKT[sl]])
            else:
                P.dma("sp", KTp[0][sl], KT[hh_], BKT[sl], writes=[BKT[sl]])
            if hh_ < 8:
                src = VA[:, hh_ * 128:(hh_ + 1) * 128].rearrange("(kb p) e -> p kb e", p=128)
            else:
                g = hh_ - 8
                src = VW[:, g * 128:(g + 1) * 128].rearrange("(kb p) e -> p kb e", p=128)
            nq4 = NKB // 4
            for q4 in range(4):
                P.dma("sp", Vb[sl][:, q4 * nq4:(q4 + 1) * nq4, :], src[:, q4 * nq4:(q4 + 1) * nq4, :], BV[sl], writes=[BV[sl]])

        state = {"unit": 0, "grp": 0, "qt": 0, "ob": 0, "oc": 0, "rc": 0, "qw": 0}
        OTB = (3, 4)
        ZTB = (5, 6)
        SSB = 7
        deferred = []

        def run_deferred(force=False):
            while deferred and (force or deferred[0][0] <= state["unit"]):
                deferred.pop(0)[1]()

        def evac_diff(h, i, c, par):
            ot, zt = banks[OTB[par]], banks[ZTB[par]]
            rc = state["rc"] % 2
            state["rc"] += 1
            P.op("dve", lambda e: e.reciprocal(out=Rc[rc], in_=zt), reads=[bankB[ZTB[par]]], writes=[BRc[rc]])
            if c == 0:
                P.op("dve", lambda e: e.tensor_tensor(out=O0, in0=ot, in1=Rc[rc], op=ALU.mult),
                     reads=[bankB[OTB[par]], BRc[rc]], writes=[BO0])
                return
            oc = state["oc"] % 2
            state["oc"] += 1
            ob = state["ob"] % 2
            state["ob"] += 1
            P.op("dve", lambda e: e.tensor_tensor(out=T1, in0=ot, in1=Rc[rc], op=ALU.mult),
                 reads=[bankB[OTB[par]], BRc[rc]], writes=[BT1])
            P.op("dve", lambda e: e.scalar_tensor_tensor(out=Oc[oc], in0=T1, scalar=neglam, in1=O0, op0=ALU.mult, op1=ALU.add),
                 reads=[BT1, Blsm, BO0], writes=[BOc[oc]])
            P.op("pool", lambda e: e.tensor_tensor(out=Osq, in0=Oc[oc], in1=Oc[oc], op=ALU.mult), reads=[BOc[oc]], writes=[BOsq])

            def later():
                P.op("pe", lambda e: e.matmul(banks[SSB], lhsT=onesf, rhs=Osq, start=True, stop=True),
                     reads=[Bonesf, BOsq], writes=[bankB[SSB]])
                P.op("dve", lambda e: e.tensor_scalar(out=Var, in0=banks[SSB], scalar1=1.0 / 128, scalar2=EPS, op0=ALU.mult, op1=ALU.add),
                     reads=[bankB[SSB]], writes=[BVar])
                P.op("act", lambda e: e.activation(out=Var, in_=Var, func=AF.Sqrt), reads=[BVar], writes=[BVar])
                P.op("dve", lambda e: e.reciprocal(out=Var, in_=Var), reads=[BVar], writes=[BVar])
                P.op("dve", lambda e: e.scalar_tensor_tensor(out=OBT[ob], in0=Oc[oc], scalar=gsubc, in1=Var, op0=ALU.mult, op1=ALU.mult),
                     reads=[BOc[oc], Bgsub, BVar], writes=[BOB[ob]])
                P.dma("pool", AOT[h][:, i * 512:(i + 1) * 512], OBT[ob], BOB[ob], reads=[BOB[ob]])
            deferred.append((state["unit"] + 6, later))

        load_head(0)
        for h in range(8):
            sl = h % 2
            load_head(h + 1)
            units = [(i, c, kb) for i in range(NT) for c in range(2) for kb in range(NKB)]
            qslot = {}

            def load_q(i):
                qs_ = state["qt"] % 3
                state["qt"] += 1
                qslot[i] = qs_
                P.dma("sp", QTt[qs_], QT[h][:, i * 512:(i + 1) * 512], BQT[qs_], writes=[BQT[qs_]])

            load_q(0)
            info = {}
            LA = 2
            n = len(units)
            for idx in range(n + LA):
                if idx < n:
                    i, c, kb = units[idx]
                    if c == 0 and kb == 0 and i + 1 < NT:
                        load_q(i + 1)
                    if kb == 0:
                        info[(i, c)] = state["grp"] % 2
                        state["grp"] += 1
                    u = state["unit"]
                    state["unit"] += 1
                    run_deferred()
                    sb = u % 3
                    eb = u % 3
                    qs_ = qslot[i]
                    P.op("pe", lambda e, sb=sb, c=c, kb=kb, qs_=qs_, sl=sl: e.matmul(
                        banks[sb], lhsT=KTp[c][sl][:, kb * 128:(kb + 1) * 128], rhs=QTt[qs_], start=True, stop=True),
                        reads=[BKT[sl], BQT[qs_]], writes=[bankB[sb]])
                    r = kb - 4 * i
                    cross = (i // (NT // 2)) != (kb // (NKB // 2))
                    if -1 <= r <= 4:
                        sn = u % 2
                        x0 = 512 - 128 * r
                        P.op("dve", lambda e, sb=sb, sn=sn, x0=x0, h=h: e.scalar_tensor_tensor(
                            out=Sn[sn], in0=banks[sb], scalar=0.125, in1=strips[h][:, x0:x0 + 512], op0=ALU.mult, op1=ALU.add),
                            reads=[bankB[sb], Bstrip[h]], writes=[BSn[sn]])
                        bcol = mvc if cross else zc
                        P.op("act", lambda e, eb=eb, sn=sn, bcol=bcol: e.activation(out=E[eb], in_=Sn[sn], func=AF.Exp, bias=bcol, scale=1.0),
                             reads=[BSn[sn], Bmv, Bzc], writes=[BE[eb]])
                    else:
                        if kb < 4 * i:
                            bcol = (clom if cross else clo)[:, h:h + 1]
                        else:
                            bcol = (chim if cross else chi)[:, h:h + 1]
                        P.op("act", lambda e, eb=eb, sb=sb, bcol=bcol: e.activation(out=E[eb], in_=banks[sb], func=AF.Exp, bias=bcol, scale=0.125),
                             reads=[bankB[sb]], writes=[BE[eb]])
                    info[idx] = eb
                j = idx - LA
                if j >= 0:
                    i, c, kb = units[j]
                    eb = info.pop(j)
                    par = info[(i, c)]
                    P.op("pe", lambda e, par=par, eb=eb, kb=kb, sl=sl: e.matmul(
                        banks[OTB[par]], lhsT=Vb[sl][:, kb, :], rhs=E[eb], start=(kb == 0), stop=(kb == NKB - 1)),
                        reads=[BE[eb], BV[sl]], writes=[bankB[OTB[par]]])
                    P.op("pe", lambda e, par=par, eb=eb, kb=kb: e.matmul(
                        banks[ZTB[par]], lhsT=onesb, rhs=E[eb], start=(kb == 0), stop=(kb == NKB - 1)),
                        reads=[BE[eb], Bonesb], writes=[bankB[ZTB[par]]])
                    if kb == NKB - 1:
                        evac_diff(h, i, c, par)
            run_deferred(force=True)
        if BSTOP < 3:
            P.barrier()
            return
        scale_w = 128 ** -0.5
        for g in range(2):
            hh_ = 8 + g
            sl = hh_ % 2
            if g == 0:
                load_head(9)
            for n4 in range(NT):
                qw = state["qw"] % 2
                state["qw"] += 1
                for hh in range(4):
                    P.dma("sp", QW[qw][:, hh, :], QT[8 + 4 * g + hh][:, n4 * 512:(n4 + 1) * 512], BQW[qw], writes=[BQW[qw]])
                for nb in range(4):
                    nq = n4 * 4 + nb
                    js = [j for j in (-1, 0, 1) if 0 <= nq + j < NKB]
                    par = state["grp"] % 2
                    state["grp"] += 1
                    for ji, j in enumerate(js):
                        kb = nq + j
                        u = state["unit"]
                        state["unit"] += 1
                        sb = u % 3
                        eb = u % 3
                        sn = u % 2
                        P.op("pe", lambda e, sb=sb, kb=kb, qw=qw, nb=nb, sl=sl: e.matmul(
                            banks[sb], lhsT=KTp[0][sl][:, kb * 128:(kb + 1) * 128], rhs=QW[qw][:, :, nb * 128:(nb + 1) * 128],
                            start=True, stop=True), reads=[BKT[sl], BQW[qw]], writes=[bankB[sb]])
                        jj = j + 1
                        P.op("dve", lambda e, sb=sb, sn=sn, g=g, jj=jj: e.scalar_tensor_tensor(
                            out=Sn[sn], in0=banks[sb], scalar=scale_w, in1=wst[g][jj].rearrange("p h q -> p (h q)"),
                            op0=ALU.mult, op1=ALU.add), reads=[bankB[sb], Bwst[g][jj]], writes=[BSn[sn]])
                        cross = (nq // (NKB // 2)) != (kb // (NKB // 2))
                        bcol = mvc if cross else zc
                        P.op("act", lambda e, eb=eb, sn=sn, bcol=bcol: e.activation(out=E[eb], in_=Sn[sn], func=AF.Exp, bias=bcol, scale=1.0),
                             reads=[BSn[sn], Bmv, Bzc], writes=[BE[eb]])
                        first, lastj = (ji == 0), (ji == len(js) - 1)
                        P.op("pe", lambda e, par=par, eb=eb, kb=kb, sl=sl, first=first, lastj=lastj: e.matmul(
                            banks[OTB[par]], lhsT=Vb[sl][:, kb, :], rhs=E[eb], start=first, stop=lastj),
                            reads=[BE[eb], BV[sl]], writes=[bankB[OTB[par]]])
                        P.op("pe", lambda e, par=par, eb=eb, first=first, lastj=lastj: e.matmul(
                            banks[ZTB[par]], lhsT=onesb, rhs=E[eb], start=first, stop=lastj),
                            reads=[BE[eb], Bonesb], writes=[bankB[ZTB[par]]])
                    rc = state["rc"] % 2
                    state["rc"] += 1
                    ob = state["ob"] % 2
                    state["ob"] += 1
                    P.op("dve", lambda e, par=par, rc=rc, g=g: e.tensor_tensor(out=Rc[rc], in0=banks[ZTB[par]],
                                                                            in1=sinkbc[g].rearrange("p h q -> p (h q)"), op=ALU.add),
                         reads=[bankB[ZTB[par]], Bsinkbc[g]], writes=[BRc[rc]])
                    P.op("dve", lambda e, rc=rc: e.reciprocal(out=Rc[rc], in_=Rc[rc]), reads=[BRc[rc]], writes=[BRc[rc]])
                    P.op("dve", lambda e, par=par, rc=rc, ob=ob: e.tensor_tensor(out=OBT[ob], in0=banks[OTB[par]], in1=Rc[rc], op=ALU.mult),
                         reads=[bankB[OTB[par]], BRc[rc]], writes=[BOB[ob]])
                    dst = AOT[8 + 4 * g:8 + 4 * g + 4, :, nq * 128:(nq + 1) * 128].rearrange("c p q -> p c q")
                    P.dma("pool", dst, OBT[ob].rearrange("p (h q) -> p h q", h=4), BOB[ob], reads=[BOB[ob]])
        P.barrier()

    def phase_C(l):
        last = (l == DEPTH - 1)
        x_src = x_d if l == 0 else XR
        x_dst = y_d if last else XR
        cf = Carver(arF)
        cb = Carver(arB)
        xC = cf.take(4 * D).rearrange("p (t d) -> p t d", t=4)
        BxC = [mkbuf(f"xC{t}") for t in range(4)]
        gbc = cf.take(D)
        Bg = mkbuf("gbcC")
        gfin = cf.take(D)
        Bgf = mkbuf("gfin")
        Rr = [cf.take(512) for _ in range(2)]
        BR = [mkbuf(f"R{i}") for i in range(2)]
        stt = [cf.take(4) for _ in range(8)]
        Bst = [mkbuf(f"stC{i}") for i in range(8)]
        AOTin = cb.take(16 * 512).rearrange("p (k n) -> p k n", k=16)
        BAOT = mkbuf("AOTin")
        XT = cb.take(16 * 512).rearrange("p (k n) -> p k n", k=16)
        BXT = mkbuf("XT")
        h2 = [cb.take(D) for _ in range(2)]
        Bh2 = [mkbuf(f"h2{i}") for i in range(2)]
        actT = cb.take(32 * 512).rearrange("p (k n) -> p k n", k=32)
        BactT = mkbuf("actT")
        ring = Ring(P, [cb.take(2048) for _ in range(8)], "rgC")
        junk = cb.take(D)
        Bjunk = mkbuf("junkC")

        P.dma("sp", gbc, n2g_d[l:l + 1, :].partition_broadcast(128), Bg, writes=[Bg])
        if last:
            P.dma("sp", gfin, fng_d[0:1, :].partition_broadcast(128), Bgf, writes=[Bgf])
        for i in range(NT):
            for b in range(4):
                for kg in range(4):
                    ring.add(WO[l, b, kg], v4)
            for hf in range(2):
                for cc in range(32):
                    ring.add(W1[l, hf * 32 + cc], v16)
                for b in range(4):
                    for kg in range(8):
                        ring.add(W2[l, b, hf * 8 + kg], v4)

        def load_tile(i):
            for k4 in range(4):
                P.dma("sp", AOTin[:, k4 * 4:(k4 + 1) * 4, :], AOT[k4 * 4:(k4 + 1) * 4, :, i * 512:(i + 1) * 512].rearrange("c p q -> p c q"),
                      BAOT, writes=[BAOT])
            for ts in range(4):
                r0 = i * 512 + ts * 128
                P.dma("sp", xC[:, ts, :], x_src[r0:r0 + 128, :], BxC[ts], writes=[BxC[ts]])

        ucount = 0
        stc = 0
        for i in range(NT):
            load_tile(i)
            for b in range(4):
                for kg in range(4):
                    wu, Bwu = ring.get(ucount)
                    ucount += 1
                    for kci in range(4):
                        kc = kg * 4 + kci
                        for ts in range(4):
                            P.op("pe", lambda e, wu=wu, kci=kci, kc=kc, ts=ts: e.matmul(
                                banks[4 + ts], lhsT=AOTin[:, kc, ts * 128:(ts + 1) * 128], rhs=wu[:, kci, :],
                                start=(kc == 0), stop=(kc == 15)), reads=[Bwu, BAOT], writes=[bankB[4 + ts]])
                for ts in range(4):
                    xs = xC[:, ts, b * 512:(b + 1) * 512]
                    P.op("dve", lambda e, xs=xs, ts=ts: e.tensor_tensor(out=xs, in0=banks[4 + ts], in1=xs, op=ALU.add),
                         reads=[bankB[4 + ts], BxC[ts]], writes=[BxC[ts]])
            for ts in range(4):
                xin = xC[:, ts, :]
                st = stt[stc % 8]
                Bs = Bst[stc % 8]
                stc += 1
                rstd = rms_rstd(xin, D, junk, Bjunk, st, Bs, [BxC[ts]])
                hs = ts % 2
                P.op("dve", lambda e, xin=xin, rstd=rstd, hs=hs: e.scalar_tensor_tensor(
                    out=h2[hs], in0=xin, scalar=rstd, in1=gbc, op0=ALU.mult, op1=ALU.mult),
                    reads=[BxC[ts], Bs, Bg], writes=[Bh2[hs]])
                transpose_block(h2[hs], Bh2[hs], XT, BXT, ts, (0, 1))
            for hf in range(2):
                for cc in range(32):
                    wu, Bwu = ring.get(ucount)
                    ucount += 1
                    fb = 2 + (cc % 2)
                    for kc in range(16):
                        P.op("pe", lambda e, wu=wu, kc=kc, fb=fb: e.matmul(
                            banks[fb], lhsT=wu[:, kc, :], rhs=XT[:, kc, :], start=(kc == 0), stop=(kc == 15)),
                            reads=[Bwu, BXT], writes=[bankB[fb]])
                    rr = cc % 2
                    P.op("act", lambda e, rr=rr, fb=fb: e.activation(out=Rr[rr], in_=banks[fb], func=AF.Relu),
                         reads=[bankB[fb]], writes=[BR[rr]])
                    P.op("pool", lambda e, rr=rr, cc=cc: e.tensor_tensor(out=actT[:, cc, :], in0=Rr[rr], in1=Rr[rr], op=ALU.mult),
                         reads=[BR[rr]], writes=[BactT])
                for b in range(4):
                    for kg in range(8):
                        wu, Bwu = ring.get(ucount)
                        ucount += 1
                        for kci in range(4):
                            kc = kg * 4 + kci
                            for ts in range(4):
                                P.op("pe", lambda e, wu=wu, kci=kci, kc=kc, ts=ts: e.matmul(
                                    banks[4 + ts], lhsT=actT[:, kc, ts * 128:(ts + 1) * 128], rhs=wu[:, kci, :],
                                    start=(kc == 0), stop=(kc == 31)), reads=[Bwu, BactT], writes=[bankB[4 + ts]])
                    for ts in range(4):
                        xs = xC[:, ts, b * 512:(b + 1) * 512]
                        P.op("dve", lambda e, xs=xs, ts=ts: e.tensor_tensor(out=xs, in0=banks[4 + ts], in1=xs, op=ALU.add),
                             reads=[bankB[4 + ts], BxC[ts]], writes=[BxC[ts]])
            for ts in range(4):
                xin = xC[:, ts, :]
                if last:
                    st = stt[stc % 8]
                    Bs = Bst[stc % 8]
                    stc += 1
                    rstd = rms_rstd(xin, D, junk, Bjunk, st, Bs, [BxC[ts]])
                    P.op("dve", lambda e, xin=xin, rstd=rstd: e.scalar_tensor_tensor(
                        out=xin, in0=xin, scalar=rstd, in1=gfin, op0=ALU.mult, op1=ALU.mult),
                        reads=[BxC[ts], Bs, Bgf], writes=[BxC[ts]])
                r0 = i * 512 + ts * 128
                P.dma("pool", x_dst[r0:r0 + 128, :], xin, BxC[ts], reads=[BxC[ts]])
        P.barrier()

    setup()
    for l in range(DEPTH):
        for nm, ph in (("A", phase_A), ("B", phase_B), ("C", phase_C)):
            if PHASES is None or f"{nm}{l}" in PHASES:
                ph(l)
    P.run()
    return nc


def _t5_bucket_np(rel):
    nb = 16
    ret = np.where(rel > 0, nb, 0)
    n = np.abs(rel)
    me = 8
    nf = np.maximum(n, 1).astype(np.float32)
    large = me + (np.log(nf / np.float32(me)) / np.float32(math.log(128 / me)) * np.float32(nb - me)).astype(np.int32)
    large = np.minimum(large, nb - 1)
    return ret + np.where(n < me, n, large)


def _static_tables():
    u = np.arange(1280)
    rel = 639 - u
    bk = _t5_bucket_np(rel.astype(np.int32))
    ohd = np.zeros((32, 1280), np.float32)
    ohw = np.zeros((64, 1280), np.float32)
    for j in range(1279):
        ohd[bk[j], j] = 1.0
        if abs(int(rel[j])) <= 128:
            ohw[bk[j], j] = 1.0
        else:
            ohw[32, j] = 1.0
    idb = np.eye(128, dtype=np.float32).astype(ml_dtypes.bfloat16)
    jm = np.ascontiguousarray(np.eye(128, dtype=np.float32)[::-1])
    return ohd, ohw, idb, jm


_NC_CACHE = {}


def kernel(x_prompt, x_sample, rel_bias, norm1_g, w_in, lambda_q1, lambda_k1, lambda_q2, lambda_k2,
           diff_subln_g, sink_logit, w_out, norm2_g, w_ff_in, w_ff_out, final_norm_g):
    f = lambda a: np.ascontiguousarray(np.asarray(a, dtype=np.float32))
    x_prompt = f(x_prompt)
    x_sample = f(x_sample)
    ohd, ohw, idb, jm = _static_tables()
    if "nc" not in _NC_CACHE:
        _NC_CACHE["nc"] = build_program()
    nc = _NC_CACHE["nc"]
    common = dict(
        rel_bias=f(rel_bias), norm1_g=f(norm1_g), w_in=f(w_in), lambda_q1=f(lambda_q1), lambda_k1=f(lambda_k1),
        lambda_q2=f(lambda_q2), lambda_k2=f(lambda_k2), diff_subln_g=f(diff_subln_g), sink_logit=f(sink_logit),
        w_out=f(w_out), norm2_g=f(norm2_g), w_ff_in=f(w_ff_in), w_ff_out=f(w_ff_out),
        final_norm_g=f(final_norm_g).reshape(1, D), idb=idb, jmat=jm, ohd=ohd, ohw=ohw)
    zeros_x = np.zeros((S, D), np.float32)
    mv0 = np.zeros((128, 1), np.float32)
    mvm = np.full((128, 1), NEGM, np.float32)
    in_maps = []
    for c in range(8):
        if c < 4:
            xc, mv = x_prompt[c], mv0
        elif c == 4:
            xc, mv = x_sample.reshape(S, D), mvm
        else:
            xc, mv = zeros_x, mv0
        m = dict(common)
        m["x"] = xc
        m["mv"] = mv
        in_maps.append(m)
    res = run_bass_kernel_spmd(nc, in_maps, core_ids=list(range(8)))
    ys = [np.asarray(res.results[c]["y"], dtype=np.float32) for c in range(5)]
    y_prompt = np.stack(ys[:4], axis=0)
    y_sample = ys[4].reshape(2, 4096, D)
    return (y_prompt, y_sample)
```

```python
import math
import numpy as np
import ml_dtypes
import concourse.bass as bass
import concourse.mybir as mybir
from concourse.bass_utils import run_bass_kernel_spmd

F32 = mybir.dt.float32
BF16 = mybir.dt.bfloat16
AF = mybir.ActivationFunctionType
ALU = mybir.AluOpType

S = 8192
D = 2048
NT = S // 512
NKB = S // 128
DEBUG_SCRATCH = False
PHASES = None
BSTOP = 9
DEPTH = 2
EPS = 1e-6
NEGM = -30000.0
VECW = 1408
COMPUTE = ("pe", "act", "dve", "pool")


class Buf:
    __slots__ = ("name", "w", "r", "rd", "sem", "cnt")

    def __init__(self, name):
        self.name = name
        self.w = None
        self.r = {}
        self.rd = []
        self.sem = None
        self.cnt = 0


_BUFS = {}


def mkbuf(name):
    b = _BUFS.get(name)
    if b is None:
        b = _BUFS[name] = Buf(name)
    return b


class Op:
    __slots__ = ("eng", "fn", "deps", "marked", "cnt", "chan", "is_dma")

    def __init__(self, eng, fn, is_dma=False, chan=None):
        self.eng = eng
        self.fn = fn
        self.deps = ()
        self.marked = False
        self.cnt = 0
        self.chan = chan
        self.is_dma = is_dma


class Prog:
    def __init__(self, nc):
        self.nc = nc
        self.streams = {e: [] for e in ("pe", "act", "dve", "pool", "sp")}
        self.sems = {}
        self.chans = []
        self.pending = {e: [] for e in self.streams}

    def _add(self, op, reads, writes):
        deps = set()
        for b in reads:
            if b.w is not None:
                deps.add(b.w)
        for b in writes:
            if b.w is not None:
                deps.add(b.w)
            deps.update(b.r.values())
            deps.update(b.rd)
        pend = self.pending[op.eng]
        if pend:
            deps.update(pend)
            self.pending[op.eng] = []
        deps.discard(op)
        op.deps = tuple(deps)
        for b in writes:
            b.w = op
            b.r = {}
            b.rd = []
        for b in reads:
            if b.w is not op:
                if op.is_dma:
                    b.rd.append(op)
                else:
                    b.r[op.eng] = op
        self.streams[op.eng].append(op)
        return op

    def op(self, eng, fn, reads=(), writes=()):
        return self._add(Op(eng, fn), reads, writes)

    def dma(self, queue, out, in_, chan, reads=(), writes=()):
        if chan.sem is None:
            chan.sem = {}
        if queue not in chan.sem:
            chan.sem[queue] = [self.nc.alloc_semaphore("c_" + queue + "_" + chan.name), 0]
            self.chans.append(chan.sem[queue])
        ent = chan.sem[queue]
        o = Op(queue, lambda e: e.dma_start(out=out, in_=in_), is_dma=True, chan=ent)
        self._add(o, reads, writes)
        ent[1] += 1
        o.cnt = ent[1]
        return o

    def barrier(self, exclude=()):
        last = []
        seen = set()
        for b in exclude:
            if b.sem:
                for ent in b.sem.values():
                    seen.add(id(ent))
        for e, s in self.streams.items():
            got = False
            for o in reversed(s):
                if o.is_dma:
                    if id(o.chan) not in seen:
                        seen.add(id(o.chan))
                        last.append(o)
                elif not got:
                    got = True
                    last.append(o)
        for e in self.pending:
            self.pending[e] = list(last)

    def finalize(self):
        for e, s in self.streams.items():
            for o in s:
                for d in o.deps:
                    if d.is_dma:
                        continue
                    if d.eng == "pe" and o.eng == "pe" and not o.is_dma:
                        continue
                    d.marked = True
        for e in COMPUTE:
            c = 0
            for o in self.streams[e]:
                if (not o.is_dma) and o.marked:
                    c += 1
                    o.cnt = c
            self.sems[e] = self.nc.alloc_semaphore("e_" + e)

    def replay(self, ename, eng):
        seen = {}
        for o in self.streams[ename]:
            need = {}
            for d in o.deps:
                if d.is_dma:
                    key = d.chan[0]
                    val = 16 * d.cnt
                else:
                    if d.eng == "pe" and ename == "pe" and not o.is_dma:
                        continue
                    key = self.sems[d.eng]
                    val = d.cnt
                k = id(key)
                if k not in need or need[k][1] < val:
                    need[k] = (key, val)
            for k, (sem, val) in need.items():
                if seen.get(k, 0) >= val:
                    continue
                seen[k] = val
                eng.wait_ge(sem, val)
            ins = o.fn(eng)
            if o.is_dma:
                ins.then_inc(o.chan[0], 16)
            elif o.marked:
                ins.then_inc(self.sems[ename], 1)

    def run(self):
        self.finalize()
        nc = self.nc
        with nc.Block() as block:
            @block.tensor
            def _(e):
                self.replay("pe", e)

            @block.scalar
            def _(e):
                self.replay("act", e)

            @block.vector
            def _(e):
                self.replay("dve", e)

            @block.gpsimd
            def _(e):
                self.replay("pool", e)

            @block.sync
            def _(e):
                self.replay("sp", e)
                for ch in self.chans:
                    if ch[1]:
                        e.wait_ge(ch[0], 16 * ch[1])


class Carver:
    def __init__(self, ap):
        self.ap = ap
        self.off = 0

    def take(self, n):
        a = self.ap[:, self.off:self.off + n]
        self.off += n
        assert self.off <= self.ap.shape[1], (self.off, self.ap.shape)
        return a


class Ring:
    def __init__(self, P, slots, name, depth=6):
        self.P = P
        self.slots = slots
        self.bufs = [mkbuf(f"{name}{i}") for i in range(len(slots))]
        self.uses = []
        self.loaded = 0
        self.depth = depth

    def add(self, src, view):
        self.uses.append((src, view))

    def get(self, k):
        R = len(self.slots)
        while self.loaded < len(self.uses) and self.loaded <= k + self.depth:
            u = self.loaded
            src, view = self.uses[u]
            self.P.dma("sp", view(self.slots[u % R]), src, self.bufs[u % R], writes=[self.bufs[u % R]])
            self.loaded += 1
        src, view = self.uses[k]
        return view(self.slots[k % R]), self.bufs[k % R]


def v16(ap):
    return ap.rearrange("p (k n) -> p k n", k=16)


def v4(ap):
    return ap.rearrange("p (k n) -> p k n", k=4)


def v4h(ap):
    return ap[:, 0:1024].rearrange("p (k n) -> p k n", k=4)


def build_program():
    _BUFS.clear()
    nc = bass.Bass("TRN2", target_bir_lowering=False)
    P = Prog(nc)

    def din(name, shape, dt=F32):
        return nc.dram_tensor(name, list(shape), dt, kind="ExternalInput").ap()

    def dscr(name, shape, dt):
        kind = "ExternalOutput" if (DEBUG_SCRATCH and name in ("QT", "KT", "VA", "VW", "AOT", "XR", "VEC")) else "Internal"
        return nc.dram_tensor(name, list(shape), dt, kind=kind).ap()

    x_d = din("x", [S, D])
    mv_d = din("mv", [128, 1])
    relb_d = din("rel_bias", [32, 16])
    n1g_d = din("norm1_g", [DEPTH, D])
    win_d = din("w_in", [DEPTH, D, 4608])
    lq1_d = din("lambda_q1", [DEPTH, 64])
    lk1_d = din("lambda_k1", [DEPTH, 64])
    lq2_d = din("lambda_q2", [DEPTH, 64])
    lk2_d = din("lambda_k2", [DEPTH, 64])
    sub_d = din("diff_subln_g", [DEPTH, 128])
    sink_d = din("sink_logit", [DEPTH, 8])
    wout_d = din("w_out", [DEPTH, D, D])
    n2g_d = din("norm2_g", [DEPTH, D])
    wf1_d = din("w_ff_in", [DEPTH, D, 8192])
    wf2_d = din("w_ff_out", [DEPTH, 8192, D])
    fng_d = din("final_norm_g", [1, D])
    idb_d = din("idb", [128, 128], BF16)
    J_d = din("jmat", [128, 128])
    ohd_d = din("ohd", [32, 1280])
    ohw_d = din("ohw", [64, 1280])
    y_d = nc.dram_tensor("y", [S, D], F32, kind="ExternalOutput").ap()

    WA = dscr("WA", [DEPTH, 26, 128, 16, 128], BF16)
    WV = dscr("WV", [DEPTH, 3, 4, 128, 4, 512], BF16)
    WO = dscr("WO", [DEPTH, 4, 4, 128, 4, 512], BF16)
    W1 = dscr("W1", [DEPTH, 64, 128, 16, 128], BF16)
    W2 = dscr("W2", [DEPTH, 4, 16, 128, 4, 512], BF16)
    QT = dscr("QT", [16, 128, S], BF16)
    KT = dscr("KT", [10, 128, S], BF16)
    VA = dscr("VA", [S, 1024], BF16)
    VW = dscr("VW", [S, 256], BF16)
    AOT = dscr("AOT", [16, 128, S], BF16)
    XR = dscr("XR", [S, D], F32)
    VEC = dscr("VEC", [16, VECW], F32)

    NF = 20480
    NB = 60672
    arF = nc.alloc_sbuf_tensor("arF", [128, NF], F32).ap()
    arB = nc.alloc_sbuf_tensor("arB", [128, NB], BF16).ap()
    idb = nc.alloc_sbuf_tensor("idb_s", [128, 128], BF16).ap()
    Jm = nc.alloc_sbuf_tensor("J_s", [128, 128], F32).ap()
    mvc = nc.alloc_sbuf_tensor("mv_s", [128, 1], F32).ap()
    zc = nc.alloc_sbuf_tensor("zc_s", [128, 1], F32).ap()
    clo = nc.alloc_sbuf_tensor("clo_s", [128, 16], F32).ap()
    chi = nc.alloc_sbuf_tensor("chi_s", [128, 16], F32).ap()
    clom = nc.alloc_sbuf_tensor("clom_s", [128, 16], F32).ap()
    chim = nc.alloc_sbuf_tensor("chim_s", [128, 16], F32).ap()
    psall = nc.alloc_psum_tensor("psall", [128, 4096], F32).ap()
    banks = [psall[:, i * 512:(i + 1) * 512] for i in range(8)]
    bankB = [mkbuf(f"bank{i}") for i in range(8)]
    Bidb, BJ, Bmv, Bzc, Bc = mkbuf("idb"), mkbuf("J"), mkbuf("mv"), mkbuf("zc"), mkbuf("cc")

    def setup():
        Bw0 = mkbuf("wprep0")
        Bw1 = mkbuf("wprep1")
        qcols = [u * 128 for u in range(8)] + [1024 + u * 128 for u in range(8)] + \
                [3072 + u * 128 for u in range(8)] + [4096, 4224]
        for l in range(DEPTH):
            Bw = Bw0 if l == 0 else Bw1
            wi = win_d[l].rearrange("(kc p) n -> p kc n", p=128)
            for u, c0 in enumerate(qcols):
                P.dma("pool", WA[l, u], wi[:, :, c0:c0 + 128], Bw)
            for b in range(3):
                c0 = 2048 + b * 512 if b < 2 else 4352
                w = 512 if b < 2 else 256
                for kg in range(4):
                    P.dma("pool", WV[l, b, kg][:, :, 0:w], wi[:, kg * 4:(kg + 1) * 4, c0:c0 + w], Bw)
            Bw = Bw1
            wo = wout_d[l].rearrange("(kc p) n -> p kc n", p=128)
            for b in range(4):
                for kg in range(4):
                    P.dma("pool", WO[l, b, kg], wo[:, kg * 4:(kg + 1) * 4, b * 512:(b + 1) * 512], Bw)
            w1 = wf1_d[l].rearrange("(kc p) n -> p kc n", p=128)
            for u in range(64):
                P.dma("pool", W1[l, u], w1[:, :, u * 128:(u + 1) * 128], Bw)
            w2 = wf2_d[l].rearrange("(kc p) n -> p kc n", p=128)
            for b in range(4):
                for kg in range(16):
                    P.dma("pool", W2[l, b, kg], w2[:, kg * 4:(kg + 1) * 4, b * 512:(b + 1) * 512], Bw)
        P.dma("sp", idb, idb_d, Bidb, writes=[Bidb])
        P.dma("sp", Jm, J_d, BJ, writes=[BJ])
        P.dma("sp", mvc, mv_d, Bmv, writes=[Bmv])
        P.op("dve", lambda e: e.memset(zc, 0.0), writes=[Bzc])
        P.dma("sp", clo, relb_d[15:16, :].partition_broadcast(128), Bc, writes=[Bc])
        Bc2 = mkbuf("cc2")
        P.dma("sp", chi, relb_d[31:32, :].partition_broadcast(128), Bc2, writes=[Bc2])
        P.op("dve", lambda e: e.tensor_scalar(out=clom, in0=clo, scalar1=mvc, scalar2=None, op0=ALU.add),
             reads=[Bc, Bmv], writes=[mkbuf("clom")])
        P.op("dve", lambda e: e.tensor_scalar(out=chim, in0=chi, scalar1=mvc, scalar2=None, op0=ALU.add),
             reads=[Bc2, Bmv], writes=[mkbuf("chim")])
        cf = Carver(arF)
        tab = cf.take(16)[0:64, :]
        ohd = cf.take(1280)[0:32, :]
        ohw = cf.take(1280)[0:64, :]
        vd = cf.take(1280)[0:8, :]
        vw = cf.take(1280)[0:8, :]
        Bt, Bohd, Bohw, Bvd, Bvw = mkbuf("tab"), mkbuf("ohd"), mkbuf("ohw"), mkbuf("vd"), mkbuf("vw")
        P.op("dve", lambda e: e.memset(tab, 0.0), writes=[Bt])
        P.op("dve", lambda e: e.memset(tab[32:33, :], NEGM), writes=[Bt])
        P.dma("sp", tab[0:32, :], relb_d, Bt, writes=[Bt])
        P.dma("sp", ohd, ohd_d, Bohd, writes=[Bohd])
        P.dma("sp", ohw, ohw_d, Bohw, writes=[Bohw])
        for pc in range(3):
            cs = slice(pc * 512, min(1280, (pc + 1) * 512))
            n = cs.stop - cs.start
            P.op("pe", lambda e, cs=cs, n=n: e.matmul(banks[0][0:8, 0:n], lhsT=tab[0:32, 0:8], rhs=ohd[:, cs],
                                                     start=True, stop=True), reads=[Bt, Bohd], writes=[bankB[0]])
            P.op("dve", lambda e, cs=cs, n=n: e.tensor_copy(out=vd[:, cs], in_=banks[0][0:8, 0:n]),
                 reads=[bankB[0]], writes=[Bvd])
            P.op("pe", lambda e, cs=cs, n=n: e.matmul(banks[1][0:8, 0:n], lhsT=tab[0:64, 8:16], rhs=ohw[:, cs],
                                                     start=True, stop=True), reads=[Bt, Bohw], writes=[bankB[1]])
            P.op("dve", lambda e, cs=cs, n=n: e.tensor_copy(out=vw[:, cs], in_=banks[1][0:8, 0:n]),
                 reads=[bankB[1]], writes=[Bvw])
        P.dma("sp", VEC[0:8, 0:1280], vd, Bvd, reads=[Bvd])
        P.dma("sp", VEC[8:16, 0:1280], vw, Bvw, reads=[Bvw])
        P.barrier(exclude=(Bw1,))

    def rms_rstd(xin, n, scr_sq, Bscr, st, Bst, reads):
        P.op("act", lambda e: e.activation(out=scr_sq, in_=xin, func=AF.Square, accum_out=st[:, 0:1]),
             reads=reads, writes=[Bscr, Bst])
        P.op("dve", lambda e: e.tensor_scalar(out=st[:, 1:2], in0=st[:, 0:1], scalar1=1.0 / n, scalar2=EPS,
                                              op0=ALU.mult, op1=ALU.add), reads=[Bst], writes=[Bst])
        P.op("act", lambda e: e.activation(out=st[:, 2:3], in_=st[:, 1:2], func=AF.Sqrt), reads=[Bst], writes=[Bst])
        P.op("dve", lambda e: e.reciprocal(out=st[:, 3:4], in_=st[:, 2:3]), reads=[Bst], writes=[Bst])
        return st[:, 3:4]

    def transpose_block(src, Bsrc, dstT, BdstT, ts, tbank_idx):
        for g8 in range(2):
            bk = tbank_idx[g8]
            tb = banks[bk].bitcast(BF16)
            for k in range(8):
                kc = g8 * 8 + k
                P.op("pe", lambda e, tb=tb, k=k, kc=kc: e.transpose(out=tb[:, k * 128:(k + 1) * 128],
                                                                   in_=src[:, kc * 128:(kc + 1) * 128], identity=idb),
                     reads=[Bsrc, Bidb], writes=[bankB[bk]])
            eng = "dve" if g8 == 0 else "act"
            dst = dstT[:, g8 * 8:(g8 + 1) * 8, ts * 128:(ts + 1) * 128]
            srcv = tb.rearrange("p (k n) -> p k n", k=8)
            if eng == "dve":
                P.op("dve", lambda e, dst=dst, srcv=srcv: e.tensor_copy(out=dst, in_=srcv), reads=[bankB[bk]], writes=[BdstT])
            else:
                P.op("act", lambda e, dst=dst, srcv=srcv: e.copy(out=dst, in_=srcv), reads=[bankB[bk]], writes=[BdstT])

    def phase_A(l):
        x_src = x_d if l == 0 else XR
        cf = Carver(arF)
        cb = Carver(arB)
        xA = [cf.take(4 * D).rearrange("p (t d) -> p t d", t=4) for _ in range(2)]
        xB = [[mkbuf(f"xA{s}{t}") for t in range(4)] for s in range(2)]
        gbc = cf.take(D)
        Bg = mkbuf("gbcA")
        stt = [[cf.take(4) for t in range(4)] for s in range(2)]
        Bst = [[mkbuf(f"stA{s}{t}") for t in range(4)] for s in range(2)]
        hb = [cb.take(D) for _ in range(2)]
        Bh = [mkbuf(f"hA{s}") for s in range(2)]
        hT = [cb.take(16 * 512).rearrange("p (k n) -> p k n", k=16) for _ in range(2)]
        BhT = [mkbuf(f"hTA{s}") for s in range(2)]
        ring = Ring(P, [cb.take(2048) for _ in range(8)], "rgA")
        stage = [cb.take(512) for _ in range(4)]
        Bstage = [mkbuf(f"stgA{s}") for s in range(4)]
        stagev = [cb.take(512) for _ in range(4)]
        Bstagev = [mkbuf(f"stgvA{s}") for s in range(4)]
        junk = cb.take(D)
        Bjunk = mkbuf("junkA")

        P.dma("sp", gbc, n1g_d[l:l + 1, :].partition_broadcast(128), Bg, writes=[Bg])
        for i in range(NT):
            for u in range(26):
                ring.add(WA[l, u], v16)
            for b in range(3):
                for kg in range(4):
                    if b < 2:
                        ring.add(WV[l, b, kg], v4)
                    else:
                        ring.add(WV[l, b, kg][:, :, 0:256], lambda ap: ap[:, 0:1024].rearrange("p (k n) -> p k n", k=4))

        def load_x(i):
            s = i % 2
            for ts in range(4):
                r0 = i * 512 + ts * 128
                P.dma("sp", xA[s][:, ts, :], x_src[r0:r0 + 128, :], xB[s][ts], writes=[xB[s][ts]])

        load_x(0)
        ucount = 0
        for i in range(NT):
            s = i % 2
            if i + 1 < NT:
                load_x(i + 1)
            for ts in range(4):
                xin = xA[s][:, ts, :]
                rstd = rms_rstd(xin, D, junk, Bjunk, stt[s][ts], Bst[s][ts], [xB[s][ts]])
                hs = ts % 2
                P.op("dve", lambda e, xin=xin, rstd=rstd, hs=hs: e.scalar_tensor_tensor(
                    out=hb[hs], in0=xin, scalar=rstd, in1=gbc, op0=ALU.mult, op1=ALU.mult),
                    reads=[xB[s][ts], Bst[s][ts], Bg], writes=[Bh[hs]])
                transpose_block(hb[hs], Bh[hs], hT[s], BhT[s], ts, (0, 1))
            for u in range(26):
                wu, Bwu = ring.get(ucount)
                ucount += 1
                fb = 2 + (u % 2)
                for kc in range(16):
                    P.op("pe", lambda e, wu=wu, kc=kc, fb=fb, s=s: e.matmul(
                        banks[fb], lhsT=wu[:, kc, :], rhs=hT[s][:, kc, :], start=(kc == 0), stop=(kc == 15)),
                        reads=[Bwu, BhT[s]], writes=[bankB[fb]])
                sg = u % 4
                P.op("act", lambda e, sg=sg, fb=fb: e.copy(out=stage[sg], in_=banks[fb]),
                     reads=[bankB[fb]], writes=[Bstage[sg]])
                if u < 8:
                    dst = QT[u]
                elif u < 16:
                    dst = KT[u - 8]
                elif u < 24:
                    dst = QT[8 + u - 16]
                else:
                    dst = KT[8 + u - 24]
                P.dma("pool", dst[:, i * 512:(i + 1) * 512], stage[sg], Bstage[sg], reads=[Bstage[sg]])
            for b in range(3):
                w = 512 if b < 2 else 256
                for kg in range(4):
                    wu, Bwu = ring.get(ucount)
                    ucount += 1
                    for kci in range(4):
                        kc = kg * 4 + kci
                        for ts in range(4):
                            P.op("pe", lambda e, wu=wu, kci=kci, kc=kc, ts=ts, w=w, s=s: e.matmul(
                                banks[4 + ts][:, 0:w], lhsT=hT[s][:, kc, ts * 128:(ts + 1) * 128], rhs=wu[:, kci, :],
                                start=(kc == 0), stop=(kc == 15)),
                                reads=[Bwu, BhT[s]], writes=[bankB[4 + ts]])
                for ts in range(4):
                    eng = "dve" if ts % 2 == 0 else "act"
                    if eng == "dve":
                        P.op("dve", lambda e, ts=ts, w=w: e.tensor_copy(out=stagev[ts][:, 0:w], in_=banks[4 + ts][:, 0:w]),
                             reads=[bankB[4 + ts]], writes=[Bstagev[ts]])
                    else:
                        P.op("act", lambda e, ts=ts, w=w: e.copy(out=stagev[ts][:, 0:w], in_=banks[4 + ts][:, 0:w]),
                             reads=[bankB[4 + ts]], writes=[Bstagev[ts]])
                    r0 = i * 512 + ts * 128
                    dst = VA[r0:r0 + 128, b * 512:(b + 1) * 512] if b < 2 else VW[r0:r0 + 128, :]
                    P.dma("pool", dst, stagev[ts][:, 0:w], Bstagev[ts], reads=[Bstagev[ts]])
        P.barrier()

    def phase_B(l):
        lam_init = 0.8 - 0.6 * math.exp(-0.3 * l)
        cf = Carver(arF)
        cb = Carver(arB)
        strips = [cf.take(1152) for _ in range(8)]
        Bstrip = [mkbuf(f"strip{h}") for h in range(8)]
        wst = [[cf.take(512).rearrange("p (h q) -> p h q", h=4) for j in range(3)] for g in range(2)]
        Bwst = [[mkbuf(f"wst{g}{j}") for j in range(3)] for g in range(2)]
        hk = cf.take(1152)
        Bhk = mkbuf("hk")
        Sn = [cf.take(512) for _ in range(2)]
        BSn = [mkbuf(f"Sn{i}") for i in range(2)]
        Rc = [cf.take(512) for _ in range(2)]
        BRc = [mkbuf(f"Rc{i}") for i in range(2)]
        O0 = cf.take(512)
        BO0 = mkbuf("O0")
        T1 = cf.take(512)
        BT1 = mkbuf("T1")
        Oc = [cf.take(512) for _ in range(2)]
        BOc = [mkbuf(f"Oc{i}") for i in range(2)]
        Osq = cf.take(512)
        BOsq = mkbuf("Osq")
        Var = cf.take(512)
        BVar = mkbuf("Var")
        onesf = cf.take(128)
        Bonesf = mkbuf("onesf")
        sinkbc = [cf.take(512).rearrange("p (h q) -> p h q", h=4) for g in range(2)]
        Bsinkbc = [mkbuf(f"sinkbc{g}") for g in range(2)]
        junkf = cf.take(128)
        Bjunkf = mkbuf("junkfB")
        gsubc = cf.take(1)
        Bgsub = mkbuf("gsub")
        sinke = cf.take(8)
        Bsink = mkbuf("sinke")
        lt = cf.take(64 * 4).rearrange("p (a d) -> p a d", a=4)
        Blt = mkbuf("lt")
        lsm = cf.take(8)
        Blsm = mkbuf("lsm")
        KTp = [[cb.take(S) for _ in range(2)] for c in range(2)]
        BKT = [mkbuf(f"KTb{i}") for i in range(2)]
        Vb = [cb.take(NKB * 128).rearrange("p (k e) -> p k e", k=NKB) for _ in range(2)]
        BV = [mkbuf(f"Vb{i}") for i in range(2)]
        QTt = [cb.take(512) for _ in range(3)]
        BQT = [mkbuf(f"QTt{i}") for i in range(3)]
        QW = [cb.take(4 * 512).rearrange("p (h q) -> p h q", h=4) for _ in range(2)]
        BQW = [mkbuf(f"QW{i}") for i in range(2)]
        Ep = [cb.take(1024) for _ in range(3)]
        E = [ep[:, 0:512] for ep in Ep]
        BE = [mkbuf(f"E{i}") for i in range(3)]
        Esum = [cb.take(512) for _ in range(3)]
        BEs = [mkbuf(f"Esum{i}") for i in range(3)]
        OBT = [cb.take(512) for _ in range(2)]
        BOB = [mkbuf(f"OB{i}") for i in range(2)]
        onesb = cb.take(128)
        Bonesb = mkbuf("onesb")

        for a, src in enumerate((lq1_d, lk1_d, lq2_d, lk2_d)):
            P.dma("sp", lt[:, a, :], src[l:l + 1, :].partition_broadcast(128), Blt, writes=[Blt])
        P.op("dve", lambda e: e.scalar_tensor_tensor(out=junkf[:, 0:64], in0=lt[:, 0, :], scalar=1.0, in1=lt[:, 1, :],
                                                     op0=ALU.mult, op1=ALU.mult, accum_out=lsm[:, 0:1]),
             reads=[Blt], writes=[Bjunkf, Blsm])
        P.op("dve", lambda e: e.scalar_tensor_tensor(out=junkf[:, 0:64], in0=lt[:, 2, :], scalar=1.0, in1=lt[:, 3, :],
                                                     op0=ALU.mult, op1=ALU.mult, accum_out=lsm[:, 1:2]),
             reads=[Blt], writes=[Bjunkf, Blsm])
        P.op("act", lambda e: e.activation(out=lsm[:, 2:4], in_=lsm[:, 0:2], func=AF.Exp), reads=[Blsm], writes=[Blsm])
        P.op("dve", lambda e: e.tensor_tensor(out=lsm[:, 4:5], in0=lsm[:, 2:3], in1=lsm[:, 3:4], op=ALU.subtract),
             reads=[Blsm], writes=[Blsm])
        P.op("dve", lambda e: e.tensor_scalar(out=lsm[:, 5:6], in0=lsm[:, 4:5], scalar1=-1.0, scalar2=-lam_init,
                                              op0=ALU.mult, op1=ALU.add), reads=[Blsm], writes=[Blsm])
        neglam = lsm[:, 5:6]
        P.dma("sp", gsubc, sub_d[l:l + 1, :].rearrange("o e -> e o"), Bgsub, writes=[Bgsub])
        P.op("dve", lambda e: e.tensor_scalar(out=gsubc, in0=gsubc, scalar1=1.0 - lam_init, scalar2=None, op0=ALU.mult),
             reads=[Bgsub], writes=[Bgsub])
        P.dma("sp", sinke, sink_d[l:l + 1, :].partition_broadcast(128), Bsink, writes=[Bsink])
        P.op("act", lambda e: e.activation(out=sinke, in_=sinke, func=AF.Exp), reads=[Bsink], writes=[Bsink])
        P.op("dve", lambda e: e.memset(onesf, 1.0), writes=[Bonesf])
        P.op("dve", lambda e: e.memset(onesb, 1.0), writes=[Bonesb])
        for g in range(2):
            P.op("dve", lambda e, g=g: e.memset(sinkbc[g], 0.0), writes=[Bsinkbc[g]])
            for hh in range(4):
                hd = 4 * g + hh
                P.op("dve", lambda e, g=g, hh=hh, hd=hd: e.tensor_scalar(out=sinkbc[g][:, hh, :], in0=sinkbc[g][:, hh, :],
                                                                       scalar1=sinke[:, hd:hd + 1], scalar2=None, op0=ALU.add),
                     reads=[Bsink, Bsinkbc[g]], writes=[Bsinkbc[g]])
        if BSTOP < 1:
            P.barrier()
            return
        for hd in range(16):
            src = bass.AP(VEC.tensor, hd * VECW, [[1, 128], [1, 1152]])
            P.dma("sp", hk, src, Bhk, writes=[Bhk])
            if hd < 8:
                for pc in range(3):
                    bk = pc
                    P.op("pe", lambda e, pc=pc, bk=bk: e.matmul(banks[bk][:, 0:384], lhsT=Jm, rhs=hk[:, pc * 384:(pc + 1) * 384],
                                                              start=True, stop=True), reads=[BJ, Bhk], writes=[bankB[bk]])
                    P.op("dve", lambda e, pc=pc, bk=bk, hd=hd: e.tensor_copy(out=strips[hd][:, pc * 384:(pc + 1) * 384],
                                                                           in_=banks[bk][:, 0:384]),
                         reads=[bankB[bk]], writes=[Bstrip[hd]])
            else:
                g, hh = (hd - 8) // 4, (hd - 8) % 4
                P.op("pe", lambda e: e.matmul(banks[3][:, 0:384], lhsT=Jm, rhs=hk[:, 384:768], start=True, stop=True),
                     reads=[BJ, Bhk], writes=[bankB[3]])
                for jj, j in enumerate((-1, 0, 1)):
                    c0 = 128 - 128 * j
                    P.op("dve", lambda e, g=g, hh=hh, jj=jj, c0=c0: e.tensor_copy(out=wst[g][jj][:, hh, :],
                                                                               in_=banks[3][:, c0:c0 + 128]),
                         reads=[bankB[3]], writes=[Bwst[g][jj]])
        if BSTOP < 2:
            P.barrier()
            return

        for sl_ in range(2):
            P.op("pool", lambda e, sl_=sl_: e.memset(KTp[0][sl_][64:128, :], 0.0), writes=[BKT[sl_]])
            P.op("pool", lambda e, sl_=sl_: e.memset(KTp[1][sl_][0:64, :], 0.0), writes=[BKT[sl_]])

        def load_head(hh_):
            sl = hh_ % 2
            if hh_ < 8:
                P.dma("sp", KTp[0][sl][0:64, :], KT[hh_][0:64, :], BKT[sl], writes=[BKT[sl]])
                P.dma("sp", KTp[1][sl][64:128, :], KT[hh_][64:128, :], BKT[sl], writes=[BKT[sl]])
            else:
                P.dma("sp", KTp[0][sl], KT[hh_], BKT[sl], writes=[BKT[sl]])
            if hh_ < 8:
                src = VA[:, hh_ * 128:(hh_ + 1) * 128].rearrange("(kb p) e -> p kb e", p=128)
            else:
                g = hh_ - 8
                src = VW[:, g * 128:(g + 1) * 128].rearrange("(kb p) e -> p kb e", p=128)
            nq4 = NKB // 4
            for q4 in range(4):
                P.dma("sp", Vb[sl][:, q4 * nq4:(q4 + 1) * nq4, :], src[:, q4 * nq4:(q4 + 1) * nq4, :], BV[sl], writes=[BV[sl]])

        state = {"unit": 0, "grp": 0, "qt": 0, "ob": 0, "oc": 0, "rc": 0, "qw": 0, "it": 0}
        OTB = (4, 5)
        ZTB = (6, 7)
        deferred = []

        def run_deferred(force=False):
            while deferred and (force or deferred[0][0] <= state["it"]):
                deferred.pop(0)[1]()

        def evac_diff(h, i, c, par):
            ot, zt = banks[OTB[par]], banks[ZTB[par]]
            ssb = ZTB[par]
            rc = state["rc"] % 2
            state["rc"] += 1
            ops = []
            for q4 in range(4):
                cs = slice(q4 * 128, (q4 + 1) * 128)
                ops.append(lambda cs=cs: P.op("dve", lambda e: e.reciprocal(out=Rc[rc][:, cs], in_=zt[:, cs]),
                                              reads=[bankB[ZTB[par]]], writes=[BRc[rc]]))
            if c == 0:
                ops.append(lambda: P.op("dve", lambda e: e.tensor_tensor(out=O0, in0=ot, in1=Rc[rc], op=ALU.mult),
                                        reads=[bankB[OTB[par]], BRc[rc]], writes=[BO0]))
            else:
                oc = state["oc"] % 2
                state["oc"] += 1
                ob = state["ob"] % 2
                state["ob"] += 1
                ops.append(lambda: P.op("dve", lambda e: e.tensor_tensor(out=T1, in0=ot, in1=Rc[rc], op=ALU.mult),
                                        reads=[bankB[OTB[par]], BRc[rc]], writes=[BT1]))
                ops.append(lambda: P.op("dve", lambda e: e.scalar_tensor_tensor(out=Oc[oc], in0=T1, scalar=neglam, in1=O0, op0=ALU.mult, op1=ALU.add),
                                        reads=[BT1, Blsm, BO0], writes=[BOc[oc]]))
                ops.append(lambda: P.op("pool", lambda e: e.tensor_tensor(out=Osq, in0=Oc[oc], in1=Oc[oc], op=ALU.mult),
                                        reads=[BOc[oc]], writes=[BOsq]))
                ops.append(lambda: None)
                ops.append(lambda: P.op("pe", lambda e: e.matmul(banks[ssb], lhsT=onesf, rhs=Osq, start=True, stop=True),
                                        reads=[Bonesf, BOsq], writes=[bankB[ssb]]))
                ops.append(lambda: P.op("dve", lambda e: e.tensor_scalar(out=Var, in0=banks[ssb], scalar1=1.0 / 128, scalar2=EPS, op0=ALU.mult, op1=ALU.add),
                                        reads=[bankB[ssb]], writes=[BVar]))
                ops.append(lambda: P.op("act", lambda e: e.activation(out=Var, in_=Var, func=AF.Sqrt), reads=[BVar], writes=[BVar]))
                for q4 in range(4):
                    cs = slice(q4 * 128, (q4 + 1) * 128)
                    ops.append(lambda cs=cs: P.op("dve", lambda e: e.reciprocal(out=Var[:, cs], in_=Var[:, cs]), reads=[BVar], writes=[BVar]))
                ops.append(lambda: P.op("dve", lambda e: e.scalar_tensor_tensor(out=OBT[ob], in0=Oc[oc], scalar=gsubc, in1=Var, op0=ALU.mult, op1=ALU.mult),
                                        reads=[BOc[oc], Bgsub, BVar], writes=[BOB[ob]]))
                ops.append(lambda: P.dma("pool", AOT[h][:, i * 512:(i + 1) * 512], OBT[ob], BOB[ob], reads=[BOB[ob]]))
            for k, fn in enumerate(ops):
                deferred.append((state["it"] + 1 + k, fn))

        HB = NKB // 2
        load_head(0)
        for h in range(8):
            sl = h % 2
            load_head(h + 1)
            items = []
            for i in range(NT):
                for c in range(2):
                    kb = 0
                    while kb < NKB:
                        near0 = -1 <= kb - 4 * i <= 4
                        near1 = -1 <= kb + 1 - 4 * i <= 4
                        if kb + 1 < NKB and not near0 and not near1 and (kb // HB) == ((kb + 1) // HB):
                            items.append((i, c, kb, 2))
                            kb += 2
                        else:
                            items.append((i, c, kb, 1))
                            kb += 1
            qslot = {}

            def load_q(i):
                qs_ = state["qt"] % 3
                state["qt"] += 1
                qslot[i] = qs_
                P.dma("sp", QTt[qs_], QT[h][:, i * 512:(i + 1) * 512], BQT[qs_], writes=[BQT[qs_]])

            load_q(0)
            info = {}
            n = len(items)
            LA = 2
            for idx in range(n + LA):
                if idx < n:
                    i, c, kb, wd = items[idx]
                    if c == 0 and kb == 0 and i + 1 < NT:
                        load_q(i + 1)
                    if kb == 0:
                        info[(i, c)] = state["grp"] % 2
                        state["grp"] += 1
                    it = state["it"]
                    state["it"] += 1
                    run_deferred()
                    ss = it % 2
                    eb = it % 3
                    qs_ = qslot[i]
                    for w_ in range(wd):
                        sb = 2 * ss + w_
                        P.op("pe", lambda e, sb=sb, c=c, kbw=kb + w_, qs_=qs_, sl=sl: e.matmul(
                            banks[sb], lhsT=KTp[c][sl][:, kbw * 128:(kbw + 1) * 128], rhs=QTt[qs_], start=True, stop=True),
                            reads=[BKT[sl], BQT[qs_]], writes=[bankB[sb]])
                    r = kb - 4 * i
                    cross = (i // (NT // 2)) != (kb // HB)
                    if wd == 1 and -1 <= r <= 4:
                        sb = 2 * ss
                        sn = it % 2
                        x0 = 512 - 128 * r
                        P.op("dve", lambda e, sb=sb, sn=sn, x0=x0, h=h: e.scalar_tensor_tensor(
                            out=Sn[sn], in0=banks[sb], scalar=0.125, in1=strips[h][:, x0:x0 + 512], op0=ALU.mult, op1=ALU.add),
                            reads=[bankB[sb], Bstrip[h]], writes=[BSn[sn]])
                        bcol = mvc if cross else zc
                        P.op("act", lambda e, eb=eb, sn=sn, bcol=bcol: e.activation(out=Ep[eb][:, 0:512], in_=Sn[sn], func=AF.Exp, bias=bcol, scale=1.0),
                             reads=[BSn[sn], Bmv, Bzc], writes=[BE[eb]])
                    else:
                        if kb < 4 * i:
                            bcol = (clom if cross else clo)[:, h:h + 1]
                        else:
                            bcol = (chim if cross else chi)[:, h:h + 1]
                        src = psall[:, 2 * ss * 512:2 * ss * 512 + 512 * wd]
                        P.op("act", lambda e, eb=eb, src=src, wd=wd, bcol=bcol: e.activation(out=Ep[eb][:, 0:512 * wd], in_=src, func=AF.Exp, bias=bcol, scale=0.125),
                             reads=[bankB[2 * ss + w_] for w_ in range(wd)], writes=[BE[eb]])
                    es = None
                    if wd == 2:
                        es = it % 3
                        P.op("dve", lambda e, eb=eb, es=es: e.tensor_tensor(out=Esum[es], in0=Ep[eb][:, 0:512], in1=Ep[eb][:, 512:1024], op=ALU.add),
                             reads=[BE[eb]], writes=[BEs[es]])
                    info[idx] = (eb, es)
                j = idx - LA
                if j >= 0:
                    i, c, kb, wd = items[j]
                    eb, es = info.pop(j)
                    par = info[(i, c)]
                    for w_ in range(wd):
                        kbw = kb + w_
                        P.op("pe", lambda e, par=par, eb=eb, kbw=kbw, w_=w_, sl=sl: e.matmul(
                            banks[OTB[par]], lhsT=Vb[sl][:, kbw, :], rhs=Ep[eb][:, w_ * 512:(w_ + 1) * 512], start=(kbw == 0), stop=(kbw == NKB - 1)),
                            reads=[BE[eb], BV[sl]], writes=[bankB[OTB[par]]])
                    first = (kb == 0)
                    lastg = (kb + wd - 1 == NKB - 1)
                    if wd == 2:
                        P.op("pe", lambda e, par=par, es=es, first=first, lastg=lastg: e.matmul(
                            banks[ZTB[par]], lhsT=onesb, rhs=Esum[es], start=first, stop=lastg),
                            reads=[BEs[es], Bonesb], writes=[bankB[ZTB[par]]])
                    else:
                        P.op("pe", lambda e, par=par, eb=eb, first=first, lastg=lastg: e.matmul(
                            banks[ZTB[par]], lhsT=onesb, rhs=Ep[eb][:, 0:512], start=first, stop=lastg),
                            reads=[BE[eb], Bonesb], writes=[bankB[ZTB[par]]])
                    if lastg:
                        evac_diff(h, i, c, par)
            run_deferred(force=True)
        if BSTOP < 3:
            P.barrier()
            return
        scale_w = 128 ** -0.5
        wunits = []
        for g in range(2):
            for nq in range(NKB):
                js = [j for j in (-1, 0, 1) if 0 <= nq + j < NKB]
                for ji, j in enumerate(js):
                    wunits.append((g, nq, j, ji == 0, ji == len(js) - 1))
        qwslot = {}

        def load_qw(g, n4):
            qw = state["qw"] % 2
            state["qw"] += 1
            qwslot[(g, n4)] = qw
            for hh in range(4):
                P.dma("sp", QW[qw][:, hh, :], QT[8 + 4 * g + hh][:, n4 * 512:(n4 + 1) * 512], BQW[qw], writes=[BQW[qw]])

        load_qw(0, 0)
        winfo = {}
        nw = len(wunits)
        LAW = 2
        for idx in range(nw + LAW):
            if idx < nw:
                g, nq, j, first, lastj = wunits[idx]
                sl = (8 + g) % 2
                n4, nb = nq // 4, nq % 4
                if first and nb == 0:
                    if g == 0 and nq == 0:
                        load_head(9)
                    nxt = (g, n4 + 1) if n4 + 1 < NT else ((g + 1, 0) if g == 0 else None)
                    if nxt is not None:
                        load_qw(*nxt)
                if first:
                    winfo[(g, nq)] = state["grp"] % 2
                    state["grp"] += 1
                kb = nq + j
                u = state["unit"]
                state["unit"] += 1
                sb = u % 3
                eb = u % 3
                sn = u % 2
                qw = qwslot[(g, n4)]
                P.op("pe", lambda e, sb=sb, kb=kb, qw=qw, nb=nb, sl=sl: e.matmul(
                    banks[sb], lhsT=KTp[0][sl][:, kb * 128:(kb + 1) * 128], rhs=QW[qw][:, :, nb * 128:(nb + 1) * 128],
                    start=True, stop=True), reads=[BKT[sl], BQW[qw]], writes=[bankB[sb]])
                jj = j + 1
                P.op("dve", lambda e, sb=sb, sn=sn, g=g, jj=jj: e.scalar_tensor_tensor(
                    out=Sn[sn], in0=banks[sb], scalar=scale_w, in1=wst[g][jj].rearrange("p h q -> p (h q)"),
                    op0=ALU.mult, op1=ALU.add), reads=[bankB[sb], Bwst[g][jj]], writes=[BSn[sn]])
                cross = (nq // HB) != (kb // HB)
                bcol = mvc if cross else zc
                P.op("act", lambda e, eb=eb, sn=sn, bcol=bcol: e.activation(out=E[eb], in_=Sn[sn], func=AF.Exp, bias=bcol, scale=1.0),
                     reads=[BSn[sn], Bmv, Bzc], writes=[BE[eb]])
                winfo[idx] = eb
            jx = idx - LAW
            if jx >= 0:
                g, nq, j, first, lastj = wunits[jx]
                sl = (8 + g) % 2
                kb = nq + j
                eb = winfo.pop(jx)
                par = winfo[(g, nq)]
                P.op("pe", lambda e, par=par, eb=eb, kb=kb, sl=sl, first=first, lastj=lastj: e.matmul(
                    banks[OTB[par]], lhsT=Vb[sl][:, kb, :], rhs=E[eb], start=first, stop=lastj),
                    reads=[BE[eb], BV[sl]], writes=[bankB[OTB[par]]])
                P.op("pe", lambda e, par=par, eb=eb, first=first, lastj=lastj: e.matmul(
                    banks[ZTB[par]], lhsT=onesb, rhs=E[eb], start=first, stop=lastj),
                    reads=[BE[eb], Bonesb], writes=[bankB[ZTB[par]]])
                if lastj:
                    rc = state["rc"] % 2
                    state["rc"] += 1
                    ob = state["ob"] % 2
                    state["ob"] += 1
                    P.op("dve", lambda e, par=par, rc=rc, g=g: e.tensor_tensor(out=Rc[rc], in0=banks[ZTB[par]],
                                                                            in1=sinkbc[g].rearrange("p h q -> p (h q)"), op=ALU.add),
                         reads=[bankB[ZTB[par]], Bsinkbc[g]], writes=[BRc[rc]])
                    P.op("dve", lambda e, rc=rc: e.reciprocal(out=Rc[rc], in_=Rc[rc]), reads=[BRc[rc]], writes=[BRc[rc]])
                    P.op("dve", lambda e, par=par, rc=rc, ob=ob: e.tensor_tensor(out=OBT[ob], in0=banks[OTB[par]], in1=Rc[rc], op=ALU.mult),
                         reads=[bankB[OTB[par]], BRc[rc]], writes=[BOB[ob]])
                    dst = AOT[8 + 4 * g:8 + 4 * g + 4, :, nq * 128:(nq + 1) * 128].rearrange("c p q -> p c q")
                    P.dma("pool", dst, OBT[ob].rearrange("p (h q) -> p h q", h=4), BOB[ob], reads=[BOB[ob]])
        P.barrier()

    def phase_C(l):
        last = (l == DEPTH - 1)
        x_src = x_d if l == 0 else XR
        x_dst = y_d if last else XR
        cf = Carver(arF)
        cb = Carver(arB)
        xC = cf.take(4 * D).rearrange("p (t d) -> p t d", t=4)
        BxC = [mkbuf(f"xC{t}") for t in range(4)]
        gbc = cf.take(D)
        Bg = mkbuf("gbcC")
        gfin = cf.take(D)
        Bgf = mkbuf("gfin")
        Rr = [cf.take(512) for _ in range(2)]
        BR = [mkbuf(f"R{i}") for i in range(2)]
        stt = [cf.take(4) for _ in range(8)]
        Bst = [mkbuf(f"stC{i}") for i in range(8)]
        AOTin = cb.take(16 * 512).rearrange("p (k n) -> p k n", k=16)
        BAOT = mkbuf("AOTin")
        XT = cb.take(16 * 512).rearrange("p (k n) -> p k n", k=16)
        BXT = mkbuf("XT")
        h2 = [cb.take(D) for _ in range(2)]
        Bh2 = [mkbuf(f"h2{i}") for i in range(2)]
        actT = cb.take(32 * 512).rearrange("p (k n) -> p k n", k=32)
        BactT = mkbuf("actT")
        ring = Ring(P, [cb.take(2048) for _ in range(8)], "rgC")
        junk = cb.take(D)
        Bjunk = mkbuf("junkC")

        P.dma("sp", gbc, n2g_d[l:l + 1, :].partition_broadcast(128), Bg, writes=[Bg])
        if last:
            P.dma("sp", gfin, fng_d[0:1, :].partition_broadcast(128), Bgf, writes=[Bgf])
        for i in range(NT):
            for b in range(4):
                for kg in range(4):
                    ring.add(WO[l, b, kg], v4)
            for hf in range(2):
                for cc in range(32):
                    ring.add(W1[l, hf * 32 + cc], v16)
                for b in range(4):
                    for kg in range(8):
                        ring.add(W2[l, b, hf * 8 + kg], v4)

        def load_tile(i):
            for k4 in range(4):
                P.dma("sp", AOTin[:, k4 * 4:(k4 + 1) * 4, :], AOT[k4 * 4:(k4 + 1) * 4, :, i * 512:(i + 1) * 512].rearrange("c p q -> p c q"),
                      BAOT, writes=[BAOT])
            for ts in range(4):
                r0 = i * 512 + ts * 128
                P.dma("sp", xC[:, ts, :], x_src[r0:r0 + 128, :], BxC[ts], writes=[BxC[ts]])

        ucount = 0
        stc = 0
        for i in range(NT):
            load_tile(i)
            for b in range(4):
                for kg in range(4):
                    wu, Bwu = ring.get(ucount)
                    ucount += 1
                    for kci in range(4):
                        kc = kg * 4 + kci
                        for ts in range(4):
                            P.op("pe", lambda e, wu=wu, kci=kci, kc=kc, ts=ts: e.matmul(
                                banks[4 + ts], lhsT=AOTin[:, kc, ts * 128:(ts + 1) * 128], rhs=wu[:, kci, :],
                                start=(kc == 0), stop=(kc == 15)), reads=[Bwu, BAOT], writes=[bankB[4 + ts]])
                for ts in range(4):
                    xs = xC[:, ts, b * 512:(b + 1) * 512]
                    P.op("dve", lambda e, xs=xs, ts=ts: e.tensor_tensor(out=xs, in0=banks[4 + ts], in1=xs, op=ALU.add),
                         reads=[bankB[4 + ts], BxC[ts]], writes=[BxC[ts]])
            for ts in range(4):
                xin = xC[:, ts, :]
                st = stt[stc % 8]
                Bs = Bst[stc % 8]
                stc += 1
                rstd = rms_rstd(xin, D, junk, Bjunk, st, Bs, [BxC[ts]])
                hs = ts % 2
                P.op("dve", lambda e, xin=xin, rstd=rstd, hs=hs: e.scalar_tensor_tensor(
                    out=h2[hs], in0=xin, scalar=rstd, in1=gbc, op0=ALU.mult, op1=ALU.mult),
                    reads=[BxC[ts], Bs, Bg], writes=[Bh2[hs]])
                transpose_block(h2[hs], Bh2[hs], XT, BXT, ts, (0, 1))
            for hf in range(2):
                for cc in range(32):
                    wu, Bwu = ring.get(ucount)
                    ucount += 1
                    fb = 2 + (cc % 2)
                    for kc in range(16):
                        P.op("pe", lambda e, wu=wu, kc=kc, fb=fb: e.matmul(
                            banks[fb], lhsT=wu[:, kc, :], rhs=XT[:, kc, :], start=(kc == 0), stop=(kc == 15)),
                            reads=[Bwu, BXT], writes=[bankB[fb]])
                    rr = cc % 2
                    P.op("act", lambda e, rr=rr, fb=fb: e.activation(out=Rr[rr], in_=banks[fb], func=AF.Relu),
                         reads=[bankB[fb]], writes=[BR[rr]])
                    P.op("pool", lambda e, rr=rr, cc=cc: e.tensor_tensor(out=actT[:, cc, :], in0=Rr[rr], in1=Rr[rr], op=ALU.mult),
                         reads=[BR[rr]], writes=[BactT])
                for b in range(4):
                    for kg in range(8):
                        wu, Bwu = ring.get(ucount)
                        ucount += 1
                        for kci in range(4):
                            kc = kg * 4 + kci
                            for ts in range(4):
                                P.op("pe", lambda e, wu=wu, kci=kci, kc=kc, ts=ts: e.matmul(
                                    banks[4 + ts], lhsT=actT[:, kc, ts * 128:(ts + 1) * 128], rhs=wu[:, kci, :],
                                    start=(kc == 0), stop=(kc == 31)), reads=[Bwu, BactT], writes=[bankB[4 + ts]])
                    for ts in range(4):
                        xs = xC[:, ts, b * 512:(b + 1) * 512]
                        P.op("dve", lambda e, xs=xs, ts=ts: e.tensor_tensor(out=xs, in0=banks[4 + ts], in1=xs, op=ALU.add),
                             reads=[bankB[4 + ts], BxC[ts]], writes=[BxC[ts]])
            for ts in range(4):
                xin = xC[:, ts, :]
                if last:
                    st = stt[stc % 8]
                    Bs = Bst[stc % 8]
                    stc += 1
                    rstd = rms_rstd(xin, D, junk, Bjunk, st, Bs, [BxC[ts]])
                    P.op("dve", lambda e, xin=xin, rstd=rstd: e.scalar_tensor_tensor(
                        out=xin, in0=xin, scalar=rstd, in1=gfin, op0=ALU.mult, op1=ALU.mult),
                        reads=[BxC[ts], Bs, Bgf], writes=[BxC[ts]])
                r0 = i * 512 + ts * 128
                P.dma("pool", x_dst[r0:r0 + 128, :], xin, BxC[ts], reads=[BxC[ts]])
        P.barrier()

    setup()
    for l in range(DEPTH):
        for nm, ph in (("A", phase_A), ("B", phase_B), ("C", phase_C)):
            if PHASES is None or f"{nm}{l}" in PHASES:
                ph(l)
    P.run()
    return nc


def _t5_bucket_np(rel):
    nb = 16
    ret = np.where(rel > 0, nb, 0)
    n = np.abs(rel)
    me = 8
    nf = np.maximum(n, 1).astype(np.float32)
    large = me + (np.log(nf / np.float32(me)) / np.float32(math.log(128 / me)) * np.float32(nb - me)).astype(np.int32)
    large = np.minimum(large, nb - 1)
    return ret + np.where(n < me, n, large)


def _static_tables():
    u = np.arange(1280)
    rel = 639 - u
    bk = _t5_bucket_np(rel.astype(np.int32))
    ohd = np.zeros((32, 1280), np.float32)
    ohw = np.zeros((64, 1280), np.float32)
    for j in range(1279):
        ohd[bk[j], j] = 1.0
        if abs(int(rel[j])) <= 128:
            ohw[bk[j], j] = 1.0
        else:
            ohw[32, j] = 1.0
    idb = np.eye(128, dtype=np.float32).astype(ml_dtypes.bfloat16)
    jm = np.ascontiguousarray(np.eye(128, dtype=np.float32)[::-1])
    return ohd, ohw, idb, jm


_NC_CACHE = {}


def kernel(x_prompt, x_sample, rel_bias, norm1_g, w_in, lambda_q1, lambda_k1, lambda_q2, lambda_k2,
           diff_subln_g, sink_logit, w_out, norm2_g, w_ff_in, w_ff_out, final_norm_g):
    f = lambda a: np.ascontiguousarray(np.asarray(a, dtype=np.float32))
    x_prompt = f(x_prompt)
    x_sample = f(x_sample)
    ohd, ohw, idb, jm = _static_tables()
    if "nc" not in _NC_CACHE:
        _NC_CACHE["nc"] = build_program()
    nc = _NC_CACHE["nc"]
    common = dict(
        rel_bias=f(rel_bias), norm1_g=f(norm1_g), w_in=f(w_in), lambda_q1=f(lambda_q1), lambda_k1=f(lambda_k1),
        lambda_q2=f(lambda_q2), lambda_k2=f(lambda_k2), diff_subln_g=f(diff_subln_g), sink_logit=f(sink_logit),
        w_out=f(w_out), norm2_g=f(norm2_g), w_ff_in=f(w_ff_in), w_ff_out=f(w_ff_out),
        final_norm_g=f(final_norm_g).reshape(1, D), idb=idb, jmat=jm, ohd=ohd, ohw=ohw)
    zeros_x = np.zeros((S, D), np.float32)
    mv0 = np.zeros((128, 1), np.float32)
    mvm = np.full((128, 1), NEGM, np.float32)
    in_maps = []
    for c in range(8):
        if c < 4:
            xc, mv = x_prompt[c], mv0
        elif c == 4:
            xc, mv = x_sample.reshape(S, D), mvm
        else:
            xc, mv = zeros_x, mv0
        m = dict(common)
        m["x"] = xc
        m["mv"] = mv
        in_maps.append(m)
    res = run_bass_kernel_spmd(nc, in_maps, core_ids=list(range(8)))
    ys = [np.asarray(res.results[c]["y"], dtype=np.float32) for c in range(5)]
    y_prompt = np.stack(ys[:4], axis=0)
    y_sample = ys[4].reshape(2, 4096, D)
    return (y_prompt, y_sample)
```
